# Optimizing a Trainium2 kernel written in Bass

```python
import math
import jax
import jax.numpy as jnp
from jax import lax
import numpy as np

D_MODEL = 1024
BATCH = 8
SEQ = 8192
DEPTH = 2

GRID_W = 64
HEAD_DIM = 64
Q_BLOCK = 128
EPS = 1e-6
A_Q_HEADS = 8
A_KV_HEADS = 2
ROPE_THETA = 10000.0
B_HEADS = 4
B_HEAD_DIM = 128
B_CHUNK = 128
C_HEADS = 8
NA_ROWS = 8
NA_COLS = 16
D_HEADS = 4
D_QK_DIM = 64
D_V_DIM = 2 * D_QK_DIM
T5_BUCKETS = 32
T5_MAX_DIST = 128
MEM_LEN = 256
CA_HEADS = 4
CA_HEAD_DIM = 128
D_FF = 2816
CONV_W = 3

EV_SIZES = (A_Q_HEADS * HEAD_DIM, A_KV_HEADS * HEAD_DIM, A_KV_HEADS * HEAD_DIM,
            B_HEADS * B_HEAD_DIM, B_HEADS * B_HEAD_DIM, B_HEADS * B_HEAD_DIM, B_HEADS * B_HEAD_DIM,
            4 * B_HEADS)
OD_SIZES = (C_HEADS * HEAD_DIM, C_HEADS * HEAD_DIM, C_HEADS * HEAD_DIM,
            2 * D_HEADS * D_QK_DIM, 2 * D_HEADS * D_QK_DIM, D_HEADS * D_V_DIM)
EV_MIX = A_Q_HEADS * HEAD_DIM + B_HEADS * B_HEAD_DIM
OD_MIX = C_HEADS * HEAD_DIM + D_HEADS * D_V_DIM

kernel_name = "hybrid_bidir_grid_encoder"


def rms_norm(x, gain):
    xf = x.astype(jnp.float32)
    y = xf * lax.rsqrt(jnp.mean(xf * xf, axis=-1, keepdims=True) + EPS)
    return (y * gain.astype(jnp.float32)).astype(x.dtype)


def heads(t, n_heads):
    b, n, _ = t.shape
    return t.reshape(b, n, n_heads, -1).transpose(0, 2, 1, 3)


def merge(t):
    b, h, n, d = t.shape
    return t.transpose(0, 2, 1, 3).reshape(b, n, h * d)


def split_cols(t, sizes):
    return jnp.split(t, np.cumsum(sizes)[:-1].tolist(), axis=-1)


def rope_1d(t, pos):
    d = t.shape[-1]
    inv = ROPE_THETA ** (-jnp.arange(0, d, 2, dtype=jnp.float32) / d)
    ang = pos.astype(jnp.float32)[:, None] * inv[None, :]
    cos, sin = jnp.cos(ang), jnp.sin(ang)
    t1, t2 = jnp.split(t.astype(jnp.float32), 2, axis=-1)
    return jnp.concatenate([t1 * cos - t2 * sin, t1 * sin + t2 * cos], axis=-1)


def axial_rope(t, rows, cols):
    half = t.shape[-1] // 2
    out = jnp.concatenate([rope_1d(t[..., :half], rows), rope_1d(t[..., half:], cols)], axis=-1)
    return out.astype(t.dtype)


def t5_bucket(rel):
    nb = T5_BUCKETS // 2
    max_exact = nb // 2
    n = jnp.abs(rel)
    log_ratio = jnp.log(jnp.maximum(n, 1).astype(jnp.float32) / max_exact) / math.log(T5_MAX_DIST / max_exact)
    large = jnp.minimum(max_exact + (log_ratio * (nb - max_exact)).astype(jnp.int32), nb - 1)
    return jnp.where(rel > 0, nb, 0) + jnp.where(n < max_exact, n, large)


def gqa_attention(q, k, v):
    b, hq, n, d = q.shape
    hkv = k.shape[1]
    rep = hq // hkv
    nb = n // Q_BLOCK
    qb = q.reshape(b, hkv, rep, nb, Q_BLOCK, d).transpose(3, 0, 1, 2, 4, 5)

    def block(qi):
        sc = jnp.einsum('bgrqd,bgkd->bgrqk', qi, k, preferred_element_type=jnp.float32)
        p = jax.nn.softmax(sc, axis=-1).astype(v.dtype)
        return jnp.einsum('bgrqk,bgkd->bgrqd', p, v)

    out = lax.map(block, qb)
    return out.transpose(1, 2, 3, 0, 4, 5).reshape(b, hq, n, d)


def mlstm_chunkwise(q, k, v, ig, lf):
    b, h, n, dk = q.shape
    dv = v.shape[-1]
    L = B_CHUNK
    nc = n // L

    def to_chunks(t):
        t = t.astype(jnp.float32)
        return jnp.moveaxis(t.reshape((b, h, nc, L) + t.shape[3:]), 2, 0)

    xs = (to_chunks(q), to_chunks(k), to_chunks(v), to_chunks(ig), to_chunks(lf))
    tril = jnp.tril(jnp.ones((L, L), dtype=bool))

    def step(carry, inp):
        c_st, n_st, m_st = carry
        qc, kc, vc, ic, fc = inp
        bcum = jnp.cumsum(fc, axis=-1)
        g = bcum[..., -1]
        log_d = jnp.where(tril, bcum[..., :, None] - bcum[..., None, :] + ic[..., None, :], -jnp.inf)
        m_inter = bcum + m_st[..., None]
        m_t = jnp.maximum(jnp.max(log_d, axis=-1), m_inter)
        dmat = jnp.exp(log_d - m_t[..., None])
        sc = jnp.einsum('bhtd,bhsd->bhts', qc, kc) * dmat
        inter = jnp.exp(m_inter - m_t)
        num = jnp.einsum('bhts,bhsv->bhtv', sc, vc) + inter[..., None] * jnp.einsum('bhtd,bhdv->bhtv', qc, c_st)
        den = jnp.sum(sc, axis=-1) + inter * jnp.einsum('bhtd,bhd->bht', qc, n_st)
        h_out = num / jnp.maximum(jnp.abs(den), jnp.exp(-m_t))[..., None]
        a = g[..., None] - bcum + ic
        m_new = jnp.maximum(g + m_st, jnp.max(a, axis=-1))
        w = jnp.exp(a - m_new[..., None])
        decay = jnp.exp(g + m_st - m_new)
        c_new = decay[..., None, None] * c_st + jnp.einsum('bhs,bhsd,bhsv->bhdv', w, kc, vc)
        n_new = decay[..., None] * n_st + jnp.einsum('bhs,bhsd->bhd', w, kc)
        return (c_new, n_new, m_new), h_out

    init = (jnp.zeros((b, h, dk, dv), jnp.float32), jnp.zeros((b, h, dk), jnp.float32),
            jnp.zeros((b, h), jnp.float32))
    _, hs = lax.scan(step, init, xs)
    return jnp.moveaxis(hs, 0, 2).reshape(b, h, n, dv)


def neighbourhood_attention(q, k, v, rpb):
    b, h, n, d = q.shape
    rows = n // GRID_W
    kr = min(NA_ROWS, rows)
    kc = NA_COLS
    kg = k.reshape(b, h, rows, GRID_W, d)
    vg = v.reshape(b, h, rows, GRID_W, d)
    qg = jnp.moveaxis(q.reshape(b, h, rows, GRID_W, d), 2, 0)
    col = jnp.arange(GRID_W)
    c0 = jnp.clip(col - kc // 2, 0, GRID_W - kc)
    in_win = (col[None, :] >= c0[:, None]) & (col[None, :] < c0[:, None] + kc)
    dc_idx = jnp.clip(col[None, :] - col[:, None] + NA_COLS - 1, 0, 2 * NA_COLS - 2)
    table = rpb.astype(jnp.float32)

    def row_block(args):
        r, qr = args
        r0 = jnp.clip(r - kr // 2, 0, rows - kr)
        kb = lax.dynamic_slice_in_dim(kg, r0, kr, axis=2).reshape(b, h, kr * GRID_W, d)
        vb = lax.dynamic_slice_in_dim(vg, r0, kr, axis=2).reshape(b, h, kr * GRID_W, d)
        dr_idx = r0 + jnp.arange(kr) - r + NA_ROWS - 1
        bias = table[:, dr_idx[:, None, None], dc_idx[None, :, :]]
        bias = jnp.where(in_win[None, None], bias, -jnp.inf)
        bias = bias.transpose(0, 2, 1, 3).reshape(h, GRID_W, kr * GRID_W)
        sc = jnp.einsum('bhqd,bhkd->bhqk', qr, kb, preferred_element_type=jnp.float32) + bias
        p = jax.nn.softmax(sc, axis=-1).astype(vb.dtype)
        return jnp.einsum('bhqk,bhkd->bhqd', p, vb)

    out = lax.map(row_block, (jnp.arange(rows), qg))
    return jnp.moveaxis(out, 0, 2).reshape(b, h, n, d)


def differential_attention(q1, q2, k1, k2, v, lam, t5_table):
    b, h, n, d = q1.shape
    nb = n // Q_BLOCK
    kpos = jnp.arange(n)
    table = t5_table.astype(jnp.float32)

    def blocks(t):
        return jnp.moveaxis(t.reshape(b, h, nb, Q_BLOCK, d), 2, 0)

    def block(args):
        i, a1, a2 = args
        qpos = i * Q_BLOCK + jnp.arange(Q_BLOCK)
        bias = table[t5_bucket(kpos[None, :] - qpos[:, None])].transpose(2, 0, 1)
        s1 = jnp.einsum('bhqd,bhkd->bhqk', a1, k1, preferred_element_type=jnp.float32) + bias
        s2 = jnp.einsum('bhqd,bhkd->bhqk', a2, k2, preferred_element_type=jnp.float32) + bias
        p = jax.nn.softmax(s1, axis=-1) - lam * jax.nn.softmax(s2, axis=-1)
        return jnp.einsum('bhqk,bhkd->bhqd', p.astype(v.dtype), v)

    out = lax.map(block, (jnp.arange(nb), blocks(q1), blocks(q2)))
    return jnp.moveaxis(out, 0, 2).reshape(b, h, n, v.shape[-1])


def even_mixer(h, w_in, gate_bias, attn_qk_gain, mlstm_gain, w_out):
    b, n, _ = h.shape
    pos = jnp.arange(n)
    rows, cols = pos // GRID_W, pos % GRID_W
    qa, ka, va, qm, km, vm, om, gates = split_cols(h @ w_in, EV_SIZES)
    qa = axial_rope(rms_norm(heads(qa, A_Q_HEADS), attn_qk_gain[0]), rows, cols) * (HEAD_DIM ** -0.5)
    ka = axial_rope(rms_norm(heads(ka, A_KV_HEADS), attn_qk_gain[1]), rows, cols)
    ya = gqa_attention(qa, ka, heads(va, A_KV_HEADS))
    qm = heads(qm, B_HEADS)
    km = heads(km, B_HEADS) * (B_HEAD_DIM ** -0.5)
    vm = heads(vm, B_HEADS)
    g = (gates.astype(jnp.float32) + gate_bias.astype(jnp.float32)).reshape(b, n, 4, B_HEADS).transpose(2, 0, 3, 1)
    h_fwd = mlstm_chunkwise(qm, km, vm, g[0], jax.nn.log_sigmoid(g[1]))
    flip = lambda t: jnp.flip(t, axis=2)
    h_bwd = flip(mlstm_chunkwise(flip(qm), flip(km), flip(vm), flip(g[2]), flip(jax.nn.log_sigmoid(g[3]))))
    hm = rms_norm(h_fwd + h_bwd, mlstm_gain.reshape(B_HEADS, 1, B_HEAD_DIM)).astype(h.dtype)
    ym = hm * jax.nn.sigmoid(heads(om, B_HEADS))
    return jnp.concatenate([merge(ya), merge(ym)], axis=-1) @ w_out


def odd_mixer(h, w_in, na_qk_gain, na_rpb, diff_qk_gain, diff_lambda, diff_gain, w_out, t5_table, layer_idx):
    b, n, _ = h.shape
    qc, kc, vc, qd, kd, vd = split_cols(h @ w_in, OD_SIZES)
    qc = rms_norm(heads(qc, C_HEADS), na_qk_gain[0]) * (HEAD_DIM ** -0.5)
    kc = rms_norm(heads(kc, C_HEADS), na_qk_gain[1])
    yc = neighbourhood_attention(qc, kc, heads(vc, C_HEADS), na_rpb)
    def pair_heads(t):
        return t.reshape(b, n, D_HEADS, 2, D_QK_DIM).transpose(3, 0, 2, 1, 4)
    q12 = rms_norm(pair_heads(qd), diff_qk_gain[0]) * (D_QK_DIM ** -0.5)
    k12 = rms_norm(pair_heads(kd), diff_qk_gain[1])
    lam_init = 0.8 - 0.6 * math.exp(-0.3 * layer_idx)
    lv = diff_lambda.astype(jnp.float32)
    lam = jnp.exp(jnp.sum(lv[0] * lv[1])) - jnp.exp(jnp.sum(lv[2] * lv[3])) + lam_init
    yd = differential_attention(q12[0], q12[1], k12[0], k12[1], heads(vd, D_HEADS), lam, t5_table)
    yd = rms_norm(yd, diff_gain.reshape(D_HEADS, 1, D_V_DIM)) * (1.0 - lam_init)
    return jnp.concatenate([merge(yc), merge(yd)], axis=-1) @ w_out


def memory_cross_attention(h, m, w_q, w_kv, qk_gain, w_o):
    q = rms_norm(heads(h @ w_q, CA_HEADS), qk_gain[0]) * (CA_HEAD_DIM ** -0.5)
    k, v = jnp.split(m @ w_kv, 2, axis=-1)
    k = rms_norm(heads(k, CA_HEADS), qk_gain[1])
    v = heads(v, CA_HEADS)
    sc = jnp.einsum('bhqd,bhkd->bhqk', q, k, preferred_element_type=jnp.float32)
    p = jax.nn.softmax(sc, axis=-1).astype(v.dtype)
    return merge(jnp.einsum('bhqk,bhkd->bhqd', p, v)) @ w_o


def conv_ffn(h, w_up, conv_w, conv_b, w_down):
    n = h.shape[1]
    u = h @ w_up
    half = CONV_W // 2
    u_pad = jnp.pad(u, ((0, 0), (half, half), (0, 0)))
    c = conv_b + u_pad[:, 0:n] * conv_w[0]
    for j in range(1, CONV_W):
        c = c + u_pad[:, j:j + n] * conv_w[j]
    gate, val = jnp.split(c, 2, axis=-1)
    return (jax.nn.silu(gate) * val) @ w_down


def setup_inputs(seed: int = 0) -> dict:
    key = jax.random.key(seed)
    ks = iter(jax.random.split(key, 40))
    ne, no = (DEPTH + 1) // 2, DEPTH // 2
    f32 = jnp.float32

    def nrm(shape, scale):
        return jax.random.normal(next(ks), shape, f32) * scale

    def gain(shape):
        return 1.0 + nrm(shape, 0.02)

    lin = jnp.linspace(3.0, 6.0, B_HEADS, dtype=f32)
    zero = jnp.zeros((B_HEADS,), f32)
    gate_base = jnp.stack([zero, lin, zero, lin]).reshape(-1)
    ev_in, od_in = sum(EV_SIZES), sum(OD_SIZES)
    return {
        "x": nrm((BATCH, SEQ, D_MODEL), 1.0),
        "mem": nrm((BATCH, MEM_LEN, D_MODEL), 1.0),
        "t5_table": nrm((T5_BUCKETS, D_HEADS), 0.1),
        "norm_mix": gain((DEPTH, D_MODEL)),
        "norm_cross": gain((DEPTH, D_MODEL)),
        "norm_mem": gain((DEPTH, D_MODEL)),
        "norm_ffn": gain((DEPTH, D_MODEL)),
        "ev_w_in": nrm((ne, D_MODEL, ev_in), D_MODEL ** -0.5),
        "ev_gate_bias": gate_base[None] + nrm((ne, 4 * B_HEADS), 0.1),
        "ev_attn_qk_gain": gain((ne, 2, HEAD_DIM)),
        "ev_mlstm_gain": gain((ne, B_HEADS * B_HEAD_DIM)),
        "ev_w_out": nrm((ne, EV_MIX, D_MODEL), EV_MIX ** -0.5),
        "od_w_in": nrm((no, D_MODEL, od_in), D_MODEL ** -0.5),
        "od_na_qk_gain": gain((no, 2, HEAD_DIM)),
        "od_na_rpb": nrm((no, C_HEADS, 2 * NA_ROWS - 1, 2 * NA_COLS - 1), 0.1),
        "od_diff_qk_gain": gain((no, 2, D_QK_DIM)),
        "od_diff_lambda": nrm((no, 4, D_QK_DIM), 0.1),
        "od_diff_gain": gain((no, D_HEADS * D_V_DIM)),
        "od_w_out": nrm((no, OD_MIX, D_MODEL), OD_MIX ** -0.5),
        "ca_w_q": nrm((DEPTH, D_MODEL, CA_HEADS * CA_HEAD_DIM), D_MODEL ** -0.5),
        "ca_w_kv": nrm((DEPTH, D_MODEL, 2 * CA_HEADS * CA_HEAD_DIM), D_MODEL ** -0.5),
        "ca_qk_gain": gain((DEPTH, 2, CA_HEAD_DIM)),
        "ca_w_o": nrm((DEPTH, CA_HEADS * CA_HEAD_DIM, D_MODEL), (CA_HEADS * CA_HEAD_DIM) ** -0.5),
        "ffn_w_up": nrm((DEPTH, D_MODEL, 2 * D_FF), D_MODEL ** -0.5),
        "ffn_conv_w": nrm((DEPTH, CONV_W, 2 * D_FF), CONV_W ** -0.5),
        "ffn_conv_b": nrm((DEPTH, 2 * D_FF), 0.02),
        "ffn_w_down": nrm((DEPTH, D_FF, D_MODEL), D_FF ** -0.5),
    }


def reference(x, mem, t5_table, norm_mix, norm_cross, norm_mem, norm_ffn,
              ev_w_in, ev_gate_bias, ev_attn_qk_gain, ev_mlstm_gain, ev_w_out,
              od_w_in, od_na_qk_gain, od_na_rpb, od_diff_qk_gain, od_diff_lambda, od_diff_gain, od_w_out,
              ca_w_q, ca_w_kv, ca_qk_gain, ca_w_o,
              ffn_w_up, ffn_conv_w, ffn_conv_b, ffn_w_down):
    for l in range(DEPTH):
        h = rms_norm(x, norm_mix[l])
        if l % 2 == 0:
            e = l // 2
            x = x + even_mixer(h, ev_w_in[e], ev_gate_bias[e], ev_attn_qk_gain[e], ev_mlstm_gain[e], ev_w_out[e])
        else:
            o = l // 2
            x = x + odd_mixer(h, od_w_in[o], od_na_qk_gain[o], od_na_rpb[o], od_diff_qk_gain[o],
                              od_diff_lambda[o], od_diff_gain[o], od_w_out[o], t5_table, l)
        x = x + memory_cross_attention(rms_norm(x, norm_cross[l]), rms_norm(mem, norm_mem[l]),
                                       ca_w_q[l], ca_w_kv[l], ca_qk_gain[l], ca_w_o[l])
        x = x + conv_ffn(rms_norm(x, norm_ffn[l]), ffn_w_up[l], ffn_conv_w[l], ffn_conv_b[l], ffn_w_down[l])
    return x
```

```python
import math
import numpy as np
import concourse.bass as bass
import concourse.mybir as mybir
from concourse.bass_utils import run_bass_kernel_spmd

F32 = mybir.dt.float32
BF16 = mybir.dt.bfloat16
AF = mybir.ActivationFunctionType
ALU = mybir.AluOpType
AX = mybir.AxisListType

D = 1024
GRID_W = 64
EPS = 1e-6
D_FF = 2816
MEM = 256
NEG = -30000.0


class T:
    def __init__(self, name, ap):
        self.name = name
        self.ap = ap
        self.w = None
        self.r = {}
        self.slot = {}

    def __getitem__(self, k):
        return self.ap[k]


class SemSlot:
    def __init__(self, sem):
        self.sem = sem
        self.cnt = 0


class Sched:
    COMPUTE = ("pe", "act", "dve", "pool")

    def __init__(self, nc):
        self.nc = nc
        self.eng = {"pe": nc.tensor, "act": nc.scalar, "dve": nc.vector, "pool": nc.gpsimd,
                    "sp": nc.sync}
        self.streams = {k: [] for k in self.eng}
        self.esem = {k: nc.alloc_semaphore("es_" + k) for k in self.COMPUTE}
        self.cnt = {k: 0 for k in self.COMPUTE}
        self.waited = {k: {} for k in self.eng}
        self.latest = {}
        self.free_slots = {}
        self.nslots = 0
        self.phase_bufs = []

    def _slot(self, queue):
        fl = self.free_slots.setdefault(queue, [])
        if fl:
            return fl.pop()
        self.nslots += 1
        return SemSlot(self.nc.alloc_semaphore("ds%d" % self.nslots))

    @staticmethod
    def _key(tok):
        return (tok[0], tok[1])

    def _need(self, stream, deps):
        out = {}
        w = self.waited[stream]
        for tok in deps:
            k = (tok[0], tok[1])
            v = tok[2]
            if w.get(k, 0) >= v:
                continue
            if out.get(k, 0) < v:
                out[k] = v
        for k, v in out.items():
            w[k] = v
        return [(k[0], k[1], v) for k, v in out.items()]

    def _deps(self, reads, writes):
        deps = []
        for b in reads:
            if b.w is not None:
                deps.append(b.w)
        for b in writes:
            if b.w is not None:
                deps.append(b.w)
            for k, v in b.r.items():
                deps.append((k[0], k[1], v))
        return deps

    def _mark(self, tok, reads, writes):
        k = (tok[0], tok[1])
        self.latest[k] = tok[2]
        for b in reads:
            b.r[k] = tok[2]
        for b in writes:
            b.w = tok
            b.r = {}

    def op(self, eng, fn, reads=(), writes=()):
        deps = self._deps(reads, writes)
        if eng == "pe":
            deps = [d for d in deps if not (d[0] == "e" and d[1] == "pe")]
        waits = self._need(eng, deps)
        self.cnt[eng] += 1
        tok = ("e", eng, self.cnt[eng])
        self.streams[eng].append(["op", fn, waits, self.cnt[eng]])
        self._mark(tok, reads, writes)

    def dma(self, out_ap, in_ap, sb, reads=(), writes=(), queue="sp", slow=False):
        deps = self._deps(reads, writes)
        waits = self._need(queue, deps)
        if queue not in sb.slot:
            if not sb.slot:
                self.phase_bufs.append(sb)
            sb.slot[queue] = self._slot(queue)
        sl = sb.slot[queue]
        sl.cnt += 1
        tok = ("d", sl, 16 * sl.cnt)
        self.streams[queue].append(["dma", (out_ap, in_ap, slow), waits, sl.sem])
        self._mark(tok, reads, writes)

    def barrier(self):
        for s in self.streams:
            deps = [(k[0], k[1], v) for k, v in self.latest.items()]
            if s == "pe":
                pass
            waits = self._need(s, deps)
            if waits:
                self.streams[s].append(["wait", None, waits, None])
        for b in self.phase_bufs:
            for q, sl in b.slot.items():
                self.free_slots.setdefault(q, []).append(sl)
            b.slot = {}
        self.phase_bufs = []

    def emit(self):
        nc = self.nc
        needed = {k: set() for k in self.COMPUTE}
        for s, lst in self.streams.items():
            for ent in lst:
                for (kind, who, v) in ent[2]:
                    if kind == "e":
                        needed[who].add(v)
        semval = {}
        for k in self.COMPUTE:
            arr = sorted(needed[k])
            semval[k] = {v: i + 1 for i, v in enumerate(arr)}
        self.final = {k: len(needed[k]) for k in self.COMPUTE}

        def lower(wt):
            kind, who, v = wt
            if kind == "e":
                return self.esem[who], semval[who][v]
            return who.sem, v

        def run(sname, engine):
            for ent in self.streams[sname]:
                kind, payload, waits, extra = ent
                lw = [lower(w) for w in waits]
                if kind == "wait":
                    for (sem, val) in lw:
                        engine.wait_ge(sem, val)
                    continue
                for (sem, val) in lw[1:]:
                    engine.wait_ge(sem, val)
                if kind == "op":
                    ins = payload(engine)
                    if lw:
                        ins._wait_ge(lw[0][0], lw[0][1])
                    if extra in semval[sname]:
                        ins.then_inc(self.esem[sname], 1)
                else:
                    out_ap, in_ap, slow = payload
                    if slow:
                        ins = engine.dma_start(out=out_ap, in_=in_ap, allow_slow_non_contiguous=True)
                    else:
                        ins = engine.dma_start(out=out_ap, in_=in_ap)
                    if lw:
                        ins._wait_ge(lw[0][0], lw[0][1])
                    ins.then_inc(extra, 16)

        with nc.Block() as block:
            @block.sync
            def _(e):
                run("sp", e)

            @block.tensor
            def _(e):
                run("pe", e)

            @block.scalar
            def _(e):
                run("act", e)

            @block.vector
            def _(e):
                run("dve", e)

            @block.gpsimd
            def _(e):
                run("pool", e)


class Arena:
    def __init__(self, nc):
        rem = nc.sbuf_bytes_remaining
        self.nwords = (rem - 2048) // 4
        self.base = nc.alloc_sbuf_tensor("arena", [128, self.nwords], F32)
        self.top = 0
        self.n = 0

    def reset(self, keep=0):
        self.top = keep

    def alloc(self, name, shape, dtype=F32):
        free = 1
        for s in shape[1:]:
            free *= s
        words = free if dtype == F32 else (free + 1) // 2
        words = (words + 7) // 8 * 8
        if self.top + words > self.nwords:
            raise RuntimeError("SBUF arena overflow at %s: need %d words, top %d of %d"
                               % (name, words, self.top, self.nwords))
        ap = self.base[0:shape[0], self.top:self.top + words]
        self.top += words
        if dtype != F32:
            ap = ap.bitcast(dtype)
        ap = ap[:, 0:free]
        if len(shape) == 3:
            ap = ap.rearrange("p (a b) -> p a b", a=shape[1])
        elif len(shape) == 4:
            ap = ap.rearrange("p (a b c) -> p a b c", a=shape[1], b=shape[2])
        self.n += 1
        return T("%s_%d" % (name, self.n), ap)


class Ctx:
    pass


def dram_in(nc, name, shape, dtype=F32):
    return nc.dram_tensor(name, list(shape), dtype, kind="ExternalInput").ap()


DBG = [False]


def dram_tmp(nc, name, shape, dtype=BF16):
    kind = "ExternalOutput" if (DBG[0] and not name.startswith("wb_")) else "Internal"
    return nc.dram_tensor(name, list(shape), dtype, kind=kind).ap()


def build(S, phases=None, dbg=None):
    nc = bass.Bass("TRN2", target_bir_lowering=False)
    c = Ctx()
    c.nc = nc
    c.S = S
    c.NT = S // 512
    c.NB = S // 128
    sch = Sched(nc)
    c.sch = sch
    ar = Arena(nc)
    c.ar = ar
    c.psall = nc.alloc_psum_tensor("psall", [128, 4096], F32)
    c.ps = [T("ps%d" % i, c.psall[:, i * 512:(i + 1) * 512]) for i in range(8)]

    I = {}
    I["x"] = dram_in(nc, "x", [S, D])
    I["mem"] = dram_in(nc, "mem", [MEM, D])
    for nm, shp in [("norm_mix", [2, D]), ("norm_cross", [2, D]), ("norm_mem", [2, D]), ("norm_ffn", [2, D]),
                    ("ev_w_in", [D, 2832]), ("ev_gate_bias", [16]), ("ev_attn_qk_gain", [2, 64]),
                    ("ev_mlstm_gain", [512]), ("ev_w_out", [D, D]),
                    ("od_w_in", [D, 3072]), ("od_na_qk_gain", [2, 64]), ("od_diff_qk_gain", [2, 64]),
                    ("od_diff_lambda", [4, 64]), ("od_diff_gain", [512]), ("od_w_out", [D, D]),
                    ("ca_w_q", [2, D, 512]), ("ca_w_kv", [2, D, 1024]), ("ca_qk_gain", [2, 2, 128]),
                    ("ca_w_o", [2, 512, D]), ("ffn_w_up", [2, D, 2 * D_FF]), ("ffn_conv_w", [2, 3, 2 * D_FF]),
                    ("ffn_conv_b", [2, 2 * D_FF]), ("ffn_w_down", [2, D_FF, D]),
                    ("c_ident", [128, 128]), ("c_cos", [S, 64]), ("c_sin", [S, 64]),
                    ("c_trif", [128, 128]), ("c_trib", [128, 128]),
                    ("c_t5", [4, 6, 128, 512]), ("c_t5far", [128, 8]),
                    ("c_nab", [8, 3, 8, 128, 512]), ("c_mmask", [8, 128, 512])]:
        I[nm] = dram_in(nc, nm, shp)
    c.I = I
    c.out = nc.dram_tensor("out", [S, D], F32, kind="ExternalOutput").ap()

    W = {}
    for nm, shp in [("ev_w_in", [D, 2832]), ("ev_w_out", [D, D]), ("od_w_in", [D, 3072]), ("od_w_out", [D, D]),
                    ("ca_w_q0", [D, 512]), ("ca_w_q1", [D, 512]), ("ca_w_kv0", [D, 1024]), ("ca_w_kv1", [D, 1024]),
                    ("ca_w_o0", [512, D]), ("ca_w_o1", [512, D]),
                    ("ffn_w_up0", [D, 2 * D_FF]), ("ffn_w_up1", [D, 2 * D_FF]),
                    ("ffn_w_down0", [D_FF, D]), ("ffn_w_down1", [D_FF, D])]:
        W[nm] = dram_tmp(nc, "wb_" + nm, shp)
    c.W = W
    X = {}
    X["x1"] = dram_tmp(nc, "x1", [S, D], F32)
    X["x2"] = dram_tmp(nc, "x2", [S, D], F32)
    X["qT"] = dram_tmp(nc, "qT", [4, 128, S])
    X["kT"] = dram_tmp(nc, "kT", [4, 128, S])
    X["v"] = dram_tmp(nc, "v", [S, 8, 128])
    X["qT2"] = dram_tmp(nc, "qT2", [4, 128, S])
    X["kT2"] = dram_tmp(nc, "kT2", [4, 128, S])
    X["v2"] = dram_tmp(nc, "v2", [S, 4, 128])
    X["qmT"] = dram_tmp(nc, "qmT", [4, 128, S])
    X["kmT"] = dram_tmp(nc, "kmT", [4, 128, S])
    X["km"] = dram_tmp(nc, "km", [S, 512])
    X["vm"] = dram_tmp(nc, "vm", [S, 4, 130])
    X["omT"] = dram_tmp(nc, "omT", [4, 128, S])
    X["grow"] = dram_tmp(nc, "grow", [8, 2, S], F32)
    X["gates"] = dram_tmp(nc, "gates", [16, S], F32)
    X["gv"] = dram_tmp(nc, "gv", [S, 48], F32)
    X["gc"] = dram_tmp(nc, "gc", [S // 128, 8], F32)
    X["hf"] = dram_tmp(nc, "hf", [S, 512], F32)
    X["yT"] = dram_tmp(nc, "yT", [8, 128, S])
    c.X = X
    c.dbg = dbg

    run = phases if phases is not None else ["w", "l0in", "gqa", "mgate", "mlstm", "wout0", "cross0", "ffn0",
                                              "l1in", "na", "diff", "wout1", "cross1", "ffn1"]
    run = [ph for ph in run if ph in PHASES]
    for ph in run:
        ar.reset()
        PHASES[ph](c)
        sch.barrier()
    sch.emit()
    return nc


PHASES = {}


def phase(name):
    def deco(f):
        PHASES[name] = f
        return f
    return deco


@phase("w")
def p_weights(c):
    sch, ar, I, W = c.sch, c.ar, c.I, c.W
    gains = ar.alloc("gains", [128, 8, 8])
    glist = [("norm_mix", 0), ("norm_mix", 1), ("norm_cross", 0), ("norm_cross", 1),
             ("norm_mem", 0), ("norm_mem", 1), ("norm_ffn", 0), ("norm_ffn", 1)]
    for gi, (nm, l) in enumerate(glist):
        sch.dma(gains[:, gi, :], I[nm][l].rearrange("(c p) -> p c", p=128), gains, writes=[gains], slow=True)
    jobs = [("ev_w_in", I["ev_w_in"], 0), ("od_w_in", I["od_w_in"], 1),
            ("ev_w_out", I["ev_w_out"], None), ("od_w_out", I["od_w_out"], None),
            ("ca_w_q0", I["ca_w_q"][0], 2), ("ca_w_q1", I["ca_w_q"][1], 3),
            ("ca_w_kv0", I["ca_w_kv"][0], 4), ("ca_w_kv1", I["ca_w_kv"][1], 5),
            ("ca_w_o0", I["ca_w_o"][0], None), ("ca_w_o1", I["ca_w_o"][1], None),
            ("ffn_w_up0", I["ffn_w_up"][0], 6), ("ffn_w_up1", I["ffn_w_up"][1], 7),
            ("ffn_w_down0", I["ffn_w_down"][0], None), ("ffn_w_down1", I["ffn_w_down"][1], None)]
    stg = [ar.alloc("wstg", [128, 2816]) for _ in range(3)]
    outb = [ar.alloc("wout", [128, 2816], BF16) for _ in range(3)]
    n = 0
    for nm, src, gi in jobs:
        K, N = src.shape
        for kc in range(K // 128):
            for c0 in range(0, N, 2816):
                cw = min(2816, N - c0)
                s_ = stg[n % 3]
                o_ = outb[n % 3]
                sch.dma(s_[:, 0:cw], src[kc * 128:(kc + 1) * 128, c0:c0 + cw], s_, writes=[s_])
                if gi is None:
                    if n % 2 == 0:
                        sch.op("dve", lambda e, o=o_, s=s_, cw=cw: e.tensor_copy(out=o[:, 0:cw], in_=s[:, 0:cw]),
                               reads=[s_], writes=[o_])
                    else:
                        sch.op("act", lambda e, o=o_, s=s_, cw=cw: e.activation(out=o[:, 0:cw], in_=s[:, 0:cw], func=AF.Copy),
                               reads=[s_], writes=[o_])
                else:
                    g = gains[:, gi, kc:kc + 1]
                    if n % 2 == 0:
                        sch.op("dve", lambda e, o=o_, s=s_, cw=cw, g=g: e.tensor_scalar(
                            out=o[:, 0:cw], in0=s[:, 0:cw], scalar1=g, scalar2=None, op0=ALU.mult),
                            reads=[s_, gains], writes=[o_])
                    else:
                        sch.op("act", lambda e, o=o_, s=s_, cw=cw, g=g: e.activation(
                            out=o[:, 0:cw], in_=s[:, 0:cw], func=AF.Copy, scale=g),
                            reads=[s_, gains], writes=[o_])
                sch.dma(W[nm][kc * 128:(kc + 1) * 128, c0:c0 + cw], o_[:, 0:cw], o_, reads=[o_], queue="pool")
                n += 1


def load_consts(c, want_ident=True):
    sch, ar, I = c.sch, c.ar, c.I
    idf = ar.alloc("idf", [128, 128])
    idb = ar.alloc("idb", [128, 128], BF16)
    sch.dma(idf[:, :], I["c_ident"], idf, writes=[idf])
    sch.op("dve", lambda e: e.tensor_copy(out=idb[:, :], in_=idf[:, :]), reads=[idf], writes=[idb])
    c.idf, c.idb = idf, idb


def load_weight(c, name, K, N, tag="wsb"):
    sch, ar = c.sch, c.ar
    w = ar.alloc(tag, [128, K // 128, N], BF16)
    src = c.W[name].rearrange("(k p) n -> p k n", p=128)
    kc = K // 128
    step = max(1, kc // 4)
    for k0 in range(0, kc, step):
        k1 = min(kc, k0 + step)
        sch.dma(w[:, k0:k1, :], src[:, k0:k1, :], w, writes=[w])
    return w


def rms_rstd(c, ss, out, n, width, extra_ln=0.0, eng_reads=()):
    sch = c.sch
    sch.op("act", lambda e: e.activation(out=out[:, 0:width], in_=ss[:, 0:width], func=AF.Ln,
                                         scale=1.0 / n, bias=c.epsb[:, 0:1]), reads=[ss, c.epsb], writes=[out])
    if extra_ln == 0.0:
        sch.op("act", lambda e: e.activation(out=out[:, 0:width], in_=out[:, 0:width], func=AF.Exp, scale=-0.5),
               reads=[out], writes=[out])
    else:
        bb = c.lnb[extra_ln]
        sch.op("act", lambda e: e.activation(out=out[:, 0:width], in_=out[:, 0:width], func=AF.Exp, scale=-0.5,
                                             bias=bb[:, 0:1]), reads=[out, bb], writes=[out])


def setup_small(c, ln_consts=()):
    sch, ar = c.sch, c.ar
    c.epsb = ar.alloc("epsb", [128, 1])
    sch.op("dve", lambda e: e.memset(c.epsb[:, :], EPS), writes=[c.epsb])
    c.lnb = {}
    for v in ln_consts:
        t = ar.alloc("lnb", [128, 1])
        sch.op("dve", lambda e, t=t, v=v: e.memset(t[:, :], v), writes=[t])
        c.lnb[v] = t


def norm_transpose(c, src, t0, ntok, hT, col0, xt, xn, ss, rs, pst, out_dt=BF16):
    sch = c.sch
    if ntok < 128:
        sch.op("dve", lambda e: e.memset(xt[:, :], 0.0), writes=[xt])
    sch.dma(xt[0:ntok, :], src[t0:t0 + ntok, :], xt, writes=[xt])
    sch.op("act", lambda e: e.activation(out=xn[:, :], in_=xt[:, :], func=AF.Square, accum_out=ss[:, 0:1]),
           reads=[xt], writes=[xn, ss])
    rms_rstd(c, ss, rs, D, 1)
    sch.op("dve", lambda e: e.tensor_scalar(out=xn[:, :], in0=xt[:, :], scalar1=rs[:, 0:1], scalar2=None, op0=ALU.mult),
           reads=[xt, rs], writes=[xn])
    pv = pst.ap.bitcast(BF16)
    for k in range(8):
        sch.op("pe", lambda e, k=k: e.transpose(out=pv[:, k * 128:(k + 1) * 128], in_=xn[:, k * 128:(k + 1) * 128],
                                                identity=c.idb[:, :]), reads=[xn, c.idb], writes=[pst])
    sch.op("act", lambda e: e.activation(out=hT[:, :, col0:col0 + ntok],
                                         in_=pv.rearrange("p (k t) -> p k t", k=8)[:, :, 0:ntok], func=AF.Copy),
           reads=[pst], writes=[hT])


def bcast_row(c, dst, src_row_ap, n):
    c.sch.dma(dst[:, 0:n], src_row_ap.partition_broadcast(128), dst, writes=[dst])


def p_in(c, layer):
    sch, ar, I, X, S = c.sch, c.ar, c.I, c.X, c.S
    even = (layer == 0)
    load_consts(c)
    setup_small(c, ln_consts=(math.log(0.125),))
    NW = 2832 if even else 3072
    wsb = load_weight(c, "ev_w_in" if even else "od_w_in", D, NW)
    NBK = S // 128
    g = ar.alloc("g", [128, 4, 64])
    gsrc = ([I["ev_attn_qk_gain"][0], I["ev_attn_qk_gain"][1]] if even else
            [I["od_na_qk_gain"][0], I["od_na_qk_gain"][1], I["od_diff_qk_gain"][0], I["od_diff_qk_gain"][1]])
    for i, gs in enumerate(gsrc):
        sch.dma(g[:, i, :], gs.partition_broadcast(128), g, writes=[g])
    if even:
        tabs = []
        gsw = ar.alloc("gsw", [128, 2, 64])
        for i in range(2):
            sch.op("dve", lambda e, i=i: e.tensor_copy(
                out=gsw[:, i, :].rearrange("p (a t d) -> p a t d", a=2, t=2),
                in_=g[:, i, :].rearrange("p (a t d) -> p a t d", a=2, t=2)[:, :, ::-1, :]), reads=[g], writes=[gsw])
        for i in range(2):
            ct = ar.alloc("ct", [128, NBK, 64])
            st = ar.alloc("st", [128, NBK, 64])
            sch.dma(ct[:, :, :], I["c_cos"].rearrange("(n p) d -> p n d", p=128), ct, writes=[ct])
            sch.dma(st[:, :, :], I["c_sin"].rearrange("(n p) d -> p n d", p=128), st, writes=[st])
            sch.op("dve", lambda e, i=i, ct=ct: e.tensor_tensor(
                out=ct[:, :, :], in0=ct[:, :, :], in1=g[:, i:i + 1, :].to_broadcast([128, NBK, 64]), op=ALU.mult),
                reads=[ct, g], writes=[ct])
            sch.op("dve", lambda e, i=i, st=st: e.tensor_tensor(
                out=st[:, :, :], in0=st[:, :, :], in1=gsw[:, i:i + 1, :].to_broadcast([128, NBK, 64]), op=ALU.mult),
                reads=[st, gsw], writes=[st])
            tabs.append((ct, st))
        gbias = ar.alloc("gbias", [16, 1])
        sch.dma(gbias[:, 0:1], I["ev_gate_bias"].rearrange("(p o) -> p o", o=1), gbias, writes=[gbias], slow=True)
    onesb = ar.alloc("onesb", [128, 64], BF16)
    sch.op("dve", lambda e: e.memset(onesb[:, :], 1.0), writes=[onesb])

    hT = [ar.alloc("hT", [128, 8, 512], BF16) for _ in range(2)]
    xt = [ar.alloc("xt", [128, D]) for _ in range(2)]
    xn = [ar.alloc("xn", [128, D], BF16) for _ in range(2)]
    ss = [ar.alloc("ss", [128, 1]) for _ in range(2)]
    rs = [ar.alloc("rs", [128, 1]) for _ in range(2)]
    sq = ar.alloc("sq", [128, 512])
    ssh = ar.alloc("ssh", [128, 8])
    rq = ar.alloc("rq", [128, 8])
    xs = ar.alloc("xs", [128, 512])
    t1 = ar.alloc("t1", [128, 512])
    t2 = ar.alloc("t2", [128, 512])
    qn = [ar.alloc("qn", [128, 512], BF16) for _ in range(2)]
    tok_out = [ar.alloc("tok_out", [128, 8, 130], BF16) for _ in range(3)]
    qTs = [ar.alloc("qTs", [128, 4, 512], BF16) for _ in range(3)]
    fms = [ar.alloc("fms", [128, 512], BF16) for _ in range(3)]
    gst = ar.alloc("gst", [16, 512])
    src = I["x"] if even else X["x1"]
    ps = c.ps
    cnt = {"ps": 0, "to": 0, "qt": 0, "fm": 0, "qn": 0}

    def nxt(k, n):
        v = cnt[k] % n
        cnt[k] += 1
        return v

    def mm_tok(pst, h, j, c0, ncols):
        for k in range(8):
            sch.op("pe", lambda e, k=k: e.matmul(pst[:, 0:ncols], lhsT=h[:, k, j * 128:(j + 1) * 128],
                                                 rhs=wsb[:, k, c0:c0 + ncols], start=(k == 0), stop=(k == 7)),
                   reads=[h, wsb], writes=[pst])

    def qk_epi(pst, nh, gi, rope, lnscale, blk, stage, j, dup):
        w = nh * 64
        sch.op("act", lambda e: e.activation(out=sq[:, 0:w], in_=pst[:, 0:w], func=AF.Square), reads=[pst], writes=[sq])
        sch.op("dve", lambda e: e.tensor_reduce(out=ssh[:, 0:nh], in_=sq[:, 0:w].rearrange("p (h d) -> p h d", h=nh),
                                                axis=AX.X, op=ALU.add), reads=[sq], writes=[ssh])
        rms_rstd(c, ssh, rq, 64, nh, extra_ln=lnscale)
        q_ = qn[nxt("qn", 2)]
        rqb = rq[:, 0:nh].unsqueeze(2).to_broadcast([128, nh, 64])
        if rope:
            ct, st = tabs[gi]
            sch.op("dve", lambda e: e.tensor_copy(
                out=xs[:, 0:w].rearrange("p (a t d) -> p a t d", t=2, d=16),
                in_=pst[:, 0:w].rearrange("p (a t d) -> p a t d", t=2, d=16)[:, :, ::-1, :]), reads=[pst], writes=[xs])
            sch.op("dve", lambda e: e.tensor_tensor(
                out=t1[:, 0:w].rearrange("p (h d) -> p h d", h=nh), in0=pst[:, 0:w].rearrange("p (h d) -> p h d", h=nh),
                in1=ct[:, blk:blk + 1, :].to_broadcast([128, nh, 64]), op=ALU.mult), reads=[pst, ct], writes=[t1])
            sch.op("pool", lambda e: e.tensor_tensor(
                out=t2[:, 0:w].rearrange("p (h d) -> p h d", h=nh), in0=xs[:, 0:w].rearrange("p (h d) -> p h d", h=nh),
                in1=st[:, blk:blk + 1, :].to_broadcast([128, nh, 64]), op=ALU.mult), reads=[xs, st], writes=[t2])
            sch.op("dve", lambda e: e.tensor_tensor(out=t1[:, 0:w], in0=t1[:, 0:w], in1=t2[:, 0:w], op=ALU.add),
                   reads=[t1, t2], writes=[t1])
        else:
            sch.op("dve", lambda e: e.tensor_tensor(
                out=t1[:, 0:w].rearrange("p (h d) -> p h d", h=nh), in0=pst[:, 0:w].rearrange("p (h d) -> p h d", h=nh),
                in1=g[:, gi:gi + 1, :].to_broadcast([128, nh, 64]), op=ALU.mult), reads=[pst, g], writes=[t1])
        if dup:
            for r in range(2):
                sch.op("dve", lambda e, r=r: e.tensor_tensor(
                    out=q_[:, 0:256].rearrange("p (h r d) -> p h r d", h=2, r=2)[:, :, r, :],
                    in0=t1[:, 0:128].rearrange("p (h d) -> p h d", h=2),
                    in1=rq[:, 0:2].unsqueeze(2).to_broadcast([128, 2, 64]), op=ALU.mult), reads=[t1, rq], writes=[q_])
            ngrp = 2
        else:
            sch.op("dve", lambda e: e.tensor_tensor(
                out=q_[:, 0:w].rearrange("p (h d) -> p h d", h=nh), in0=t1[:, 0:w].rearrange("p (h d) -> p h d", h=nh),
                in1=rqb, op=ALU.mult), reads=[t1, rq], writes=[q_])
            ngrp = w // 128
        pt = ps[4 + nxt("ps", 4)]
        pv = pt.ap.bitcast(BF16)
        for gp in range(ngrp):
            sch.op("pe", lambda e, gp=gp: e.transpose(out=pv[:, gp * 128:(gp + 1) * 128], in_=q_[:, gp * 128:(gp + 1) * 128],
                                                      identity=c.idb[:, :]), reads=[q_, c.idb], writes=[pt])
        sch.op("act", lambda e: e.activation(out=stage[:, 0:ngrp, j * 128:(j + 1) * 128],
                                             in_=pv[:, 0:ngrp * 128].rearrange("p (g t) -> p g t", g=ngrp), func=AF.Copy),
               reads=[pt], writes=[stage])

    for ti in range(c.NT):
        t0 = ti * 512
        h = hT[ti % 2]
        for j in range(4):
            norm_transpose(c, src, t0 + j * 128, 128, h, j * 128, xt[j % 2], xn[j % 2], ss[j % 2], rs[j % 2], ps[j % 2])
        if even:
            blocks = [("qk", 0, 8, 0, True, math.log(0.125), "qT", 0, False),
                      ("qk", 512, 2, 1, True, 0.0, "kT", 0, True),
                      ("v", 640, 2, None, None, None, "v", 0, None),
                      ("tokp", 1280, 512, None, None, None, "km", 0, None),
                      ("vm", 1792, 4, None, None, None, "vm", 0, None),
                      ]
            fmb = ([(768 + 128 * i, "qmT", i, 1.0) for i in range(4)] + [(1280 + 128 * i, "kmT", i, 1.0) for i in range(4)]
                   + [(2304 + 128 * i, "omT", i, 1.0) for i in range(4)])
        else:
            blocks = [("qk", 0, 8, 0, False, math.log(0.125), "qT", 0, False),
                      ("qk", 512, 8, 1, False, 0.0, "kT", 0, False),
                      ("v", 1024, 8, None, None, None, "v", 0, None),
                      ("qk", 1536, 8, 2, False, math.log(0.125), "qT2", 0, False),
                      ("qk", 2048, 8, 3, False, 0.0, "kT2", 0, False),
                      ("tokp", 2560, 512, None, None, None, "v2", 0, None)]
            fmb = []
        stages = {}
        for (kind, c0, a, gi, rope, lns, dst, _, dup) in blocks:
            if kind == "qk":
                stages[(dst, c0)] = qTs[nxt("qt", 3)]
            for j in range(4):
                blk = (t0 // 128) + j
                pst = ps[2 + (cnt["to"] % 2)]
                cnt["to"] += 1
                r0 = t0 + j * 128
                if kind == "qk":
                    mm_tok(pst, h, j, c0, a * 64)
                    qk_epi(pst, a, gi, rope, lns, blk, stages[(dst, c0)], j, dup)
                elif kind == "v":
                    mm_tok(pst, h, j, c0, a * 64)
                    cnt["to2"] = cnt.get("to2", 0) + 1
                    to = tok_out[cnt["to2"] % 3]
                    sch.op("act", lambda e, to=to, pst=pst, a=a: e.activation(
                        out=to[:, 0:a, 0:64], in_=pst[:, 0:a * 64].rearrange("p (h d) -> p h d", h=a), func=AF.Copy),
                        reads=[pst], writes=[to])
                    sch.op("pool", lambda e, to=to, a=a: e.tensor_copy(
                        out=to[:, 0:a, 64:128], in_=onesb[:, :].unsqueeze(1).to_broadcast([128, a, 64])),
                        reads=[onesb], writes=[to])
                    sch.dma(X["v"][r0:r0 + 128, 0:a, :], to[:, 0:a, 0:128], to, reads=[to], queue="pool")
                elif kind == "vm":
                    mm_tok(pst, h, j, c0, 512)
                    cnt["to2"] = cnt.get("to2", 0) + 1
                    to = tok_out[cnt["to2"] % 3]
                    sch.op("act", lambda e, to=to, pst=pst: e.activation(
                        out=to[:, 0:4, 0:128], in_=pst[:, 0:512].rearrange("p (h d) -> p h d", h=4), func=AF.Copy),
                        reads=[pst], writes=[to])
                    sch.op("pool", lambda e, to=to: e.memset(to[:, 0:4, 128:129], 1.0), writes=[to])
                    sch.op("pool", lambda e, to=to: e.memset(to[:, 0:4, 129:130], 0.0), writes=[to])
                    sch.dma(X["vm"][r0:r0 + 128, :, :], to[:, 0:4, :], to, reads=[to], queue="pool")
                elif kind == "tokp":
                    mm_tok(pst, h, j, c0, 512)
                    cnt["to2"] = cnt.get("to2", 0) + 1
                    to = tok_out[cnt["to2"] % 3]
                    tf = to.ap.rearrange("p a b -> p (a b)")
                    scl = (128.0 ** -0.5) if dst == "km" else 1.0
                    sch.op("act", lambda e, tf=tf, pst=pst, scl=scl: e.activation(
                        out=tf[:, 0:512], in_=pst[:, 0:512], func=AF.Copy, scale=scl), reads=[pst], writes=[to])
                    dd = X[dst] if dst != "v2" else X["v2"]
                    sch.dma(dd[r0:r0 + 128, :] if dst != "v2" else dd[r0:r0 + 128, :, :].rearrange("t h d -> t (h d)"),
                            tf[:, 0:512], to, reads=[to], queue="pool")
            if kind == "qk":
                stg = stages[(dst, c0)]
                ng = 2 if dup else (a * 64) // 128
                sch.dma(X[dst][0:ng, :, t0:t0 + 512].rearrange("g p t -> p g t"), stg[:, 0:ng, :], stg, reads=[stg],
                        queue="pool")
        for (c0, dst, gi, _) in fmb:
            pst = ps[2 + (cnt["to"] % 2)]
            cnt["to"] += 1
            for k in range(8):
                sch.op("pe", lambda e, k=k, pst=pst, c0=c0, h=h: e.matmul(pst[:, 0:512], lhsT=wsb[:, k, c0:c0 + 128],
                                                                          rhs=h[:, k, :], start=(k == 0), stop=(k == 7)),
                       reads=[h, wsb], writes=[pst])
            f_ = fms[nxt("fm", 3)]
            scl = (128.0 ** -0.5) if dst == "kmT" else 1.0
            sch.op("act", lambda e, f_=f_, pst=pst, scl=scl: e.activation(out=f_[:, :], in_=pst[:, 0:512], func=AF.Copy,
                                                                          scale=scl), reads=[pst], writes=[f_])
            sch.dma(X[dst][gi, :, t0:t0 + 512], f_[:, :], f_, reads=[f_], queue="pool")
        if even:
            pst = ps[2 + (cnt["to"] % 2)]
            cnt["to"] += 1
            for k in range(8):
                sch.op("pe", lambda e, k=k, pst=pst, h=h: e.matmul(pst[0:16, 0:512], lhsT=wsb[:, k, 2816:2832],
                                                                   rhs=h[:, k, :], start=(k == 0), stop=(k == 7)),
                       reads=[h, wsb], writes=[pst])
            sch.op("act", lambda e, pst=pst: e.activation(out=gst[:, :], in_=pst[0:16, 0:512], func=AF.Identity,
                                                          bias=gbias[:, 0:1]), reads=[pst, gbias], writes=[gst])
            sch.dma(X["gates"][:, t0:t0 + 512], gst[:, :], gst, reads=[gst], queue="pool")


@phase("l0in")
def p_l0in(c):
    p_in(c, 0)


@phase("l1in")
def p_l1in(c):
    p_in(c, 1)


def p_attn(c, mode):
    sch, ar, I, X, S = c.sch, c.ar, c.I, c.X, c.S
    NB, NT = c.NB, c.NT
    ps = c.ps
    load_consts(c)
    setup_small(c)
    qsrc, ksrc = (X["qT2"], X["kT2"]) if mode == "diff" else (X["qT"], X["kT"])
    kTs = ar.alloc("kTs", [128, S], BF16)
    nv = 1 if mode != "na" else 2
    vsb = [ar.alloc("vsb", [128, NB, 128], BF16) for _ in range(nv)]
    qs = [ar.alloc("qs", [128, 512], BF16) for _ in range(2)]
    pT = [ar.alloc("pT", [128, 1024], BF16) for _ in range(3)]
    rc = ar.alloc("rc", [128, 512])
    rcs = [ar.alloc("rcs", [128, 512]) for _ in range(2)]
    ysb = [ar.alloc("ysb", [128, 512], BF16) for _ in range(2)]
    if mode == "diff":
        onesk = ar.alloc("onesk", [128, 128], BF16)
        sch.op("dve", lambda e: e.memset(onesk[:, :], 1.0), writes=[onesk])
        onesf = ar.alloc("onesf", [128, 128])
        sch.op("dve", lambda e: e.memset(onesf[:, :], 1.0), writes=[onesf])
        pacc = [[ar.alloc("pacc", [128, 512]) for _ in range(2)] for _ in range(2)]
        t5far = ar.alloc("t5far", [128, 8])
        sch.dma(t5far[:, :], I["c_t5far"], t5far, writes=[t5far])
        bstg = ar.alloc("bstg", [128, 512])
        bias_sb = ar.alloc("bias_sb", [128, 6, 512], BF16)
        lam_init = 0.8 - 0.6 * math.exp(-0.3 * 1)
        lv = ar.alloc("lv", [128, 4, 64])
        for i in range(4):
            sch.dma(lv[:, i, :], I["od_diff_lambda"][i].partition_broadcast(128), lv, writes=[lv])
        lt = ar.alloc("lt", [128, 2, 64])
        lsum = ar.alloc("lsum", [128, 2])
        nlam = ar.alloc("nlam", [128, 1])
        sch.op("dve", lambda e: e.tensor_tensor(out=lt[:, :, :], in0=lv[:, 0:4:2, :], in1=lv[:, 1:4:2, :], op=ALU.mult),
               reads=[lv], writes=[lt])
        sch.op("dve", lambda e: e.tensor_reduce(out=lsum[:, :], in_=lt[:, :, :], axis=AX.X, op=ALU.add),
               reads=[lt], writes=[lsum])
        sch.op("act", lambda e: e.activation(out=lsum[:, :], in_=lsum[:, :], func=AF.Exp), reads=[lsum], writes=[lsum])
        sch.op("dve", lambda e: e.tensor_scalar(out=nlam[:, :], in0=lsum[:, 1:2], scalar1=lsum[:, 0:1], scalar2=-lam_init,
                                                op0=ALU.subtract, op1=ALU.add), reads=[lsum], writes=[nlam])
        dgain = ar.alloc("dgain", [128, 4])
        sch.dma(dgain[:, :], I["od_diff_gain"].rearrange("(h p) -> p h", p=128), dgain, writes=[dgain], slow=True)
        sch.op("dve", lambda e: e.tensor_scalar(out=dgain[:, :], in0=dgain[:, :], scalar1=(1.0 - lam_init), scalar2=None,
                                                op0=ALU.mult), reads=[dgain], writes=[dgain])
        y1 = ar.alloc("y1", [128, 512])
        y2 = ar.alloc("y2", [128, 512])
        ysq = ar.alloc("ysq", [128, 512], BF16)
    if mode == "na":
        bstg = ar.alloc("bstg", [128, 512])
        bias_na = [ar.alloc("bias_na", [128, 24, 512], BF16) for _ in range(2)]
    step = 0
    for grp in range(4):
        kg = (grp // 2) if mode == "gqa" else grp
        sch.dma(kTs[:, :], ksrc[kg], kTs, writes=[kTs])
        if mode == "gqa":
            sch.dma(vsb[0][:, :, :], X["v"][:, kg, :].rearrange("(n p) d -> p n d", p=128), vsb[0], writes=[vsb[0]])
        elif mode == "na":
            for r in range(2):
                sch.dma(vsb[r][:, :, :], X["v"][:, 2 * grp + r, :].rearrange("(n p) d -> p n d", p=128), vsb[r],
                        writes=[vsb[r]])
                for var in range(3):
                    for o in range(8):
                        sch.dma(bstg[:, :], I["c_nab"][2 * grp + r, var, o], bstg, writes=[bstg])
                        sch.op("dve", lambda e, r=r, var=var, o=o: e.tensor_copy(out=bias_na[r][:, var * 8 + o, :],
                                                                               in_=bstg[:, :]),
                               reads=[bstg], writes=[bias_na[r]])
        else:
            sch.dma(vsb[0][:, :, :], X["v2"][:, grp, :].rearrange("(n p) d -> p n d", p=128), vsb[0], writes=[vsb[0]])
            for o in range(6):
                sch.dma(bstg[:, :], I["c_t5"][grp, o], bstg, writes=[bstg])
                sch.op("dve", lambda e, o=o: e.tensor_copy(out=bias_sb[:, o, :], in_=bstg[:, :]),
                       reads=[bstg], writes=[bias_sb])
        steps = []
        for qt in range(NT):
            if mode == "na":
                kbs = [kb for kb in range(4 * qt - 2, 4 * qt + 6) if 0 <= kb < NB]
            else:
                kbs = list(range(NB))
            for ki, kb in enumerate(kbs):
                steps.append((qt, kb, ki == 0, ki == len(kbs) - 1))

        def load_q(qt):
            q_ = qs[qt % 2]
            sch.dma(q_[:, :], qsrc[grp, :, qt * 512:qt * 512 + 512], q_, writes=[q_])

        def banks(i):
            par = (step0 + i) % 2
            return ps[2 * par], ps[2 * par + 1], c.psall[:, 1024 * par:1024 * par + 1024]

        def emit_qk(i):
            qt, kb, first, last = steps[i]
            if first and qt + 1 < NT:
                load_q(qt + 1)
            q_ = qs[qt % 2]
            pa, pb, _ = banks(i)
            ks = slice(kb * 128, (kb + 1) * 128)
            need_bias = (mode == "na") or (mode == "diff" and -1 <= kb - 4 * qt <= 4)
            sch.op("pe", lambda e, pa=pa, q_=q_, ks=ks, nb=need_bias: e.matmul(
                pa[:, :], lhsT=kTs[0:64, ks], rhs=q_[0:64, :], start=True, stop=not nb),
                reads=[kTs, q_], writes=[pa])
            sch.op("pe", lambda e, pb=pb, q_=q_, ks=ks, nb=need_bias: e.matmul(
                pb[:, :], lhsT=kTs[64:128, ks], rhs=q_[64:128, :], start=True, stop=not nb),
                reads=[kTs, q_], writes=[pb])
            if need_bias:
                if mode == "na":
                    var = 0 if qt == 0 else (2 if qt == NT - 1 else 1)
                    o = kb - (4 * qt - 2)
                    ba, bb, rd = bias_na[0][:, var * 8 + o, :], bias_na[1][:, var * 8 + o, :], [bias_na[0], bias_na[1]]
                else:
                    o = kb - 4 * qt + 1
                    ba = bb = bias_sb[:, o, :]
                    rd = [bias_sb]
                sch.op("pe", lambda e, pa=pa, ba=ba: e.matmul(pa[:, :], lhsT=c.idb[:, :], rhs=ba, start=False, stop=True),
                       reads=[c.idb] + rd, writes=[pa])
                sch.op("pe", lambda e, pb=pb, bb=bb: e.matmul(pb[:, :], lhsT=c.idb[:, :], rhs=bb, start=False, stop=True),
                       reads=[c.idb] + rd, writes=[pb])

        def accs(qt):
            if mode == "diff":
                return ps[4], ps[5], ps[6], ps[7]
            return ps[4 + 2 * (qt % 2)], ps[5 + 2 * (qt % 2)], None, None

        def emit_exp_pv(i):
            qt, kb, first, last = steps[i]
            pa, pb, pab = banks(i)
            p_ = pT[(step0 + i) % 3]
            oA, oB, sA, sB = accs(qt)
            need_bias = (mode == "na") or (mode == "diff" and -1 <= kb - 4 * qt <= 4)
            if mode == "diff" and not need_bias:
                fb = t5far[:, grp:grp + 1] if kb < 4 * qt else t5far[:, 4 + grp:5 + grp]
                sch.op("act", lambda e, p_=p_, pab=pab, fb=fb: e.activation(out=p_[:, :], in_=pab, func=AF.Exp, bias=fb),
                       reads=[pa, pb, t5far], writes=[p_])
            else:
                sch.op("act", lambda e, p_=p_, pab=pab: e.activation(out=p_[:, :], in_=pab, func=AF.Exp),
                       reads=[pa, pb], writes=[p_])
            va = vsb[0][:, kb, :]
            vb = vsb[nv - 1][:, kb, :]
            sch.op("pe", lambda e, oA=oA, va=va, p_=p_, first=first, last=last: e.matmul(
                oA[:, :], lhsT=va, rhs=p_[:, 0:512], start=first, stop=last), reads=[vsb[0], p_], writes=[oA])
            sch.op("pe", lambda e, oB=oB, vb=vb, p_=p_, first=first, last=last: e.matmul(
                oB[:, :], lhsT=vb, rhs=p_[:, 512:1024], start=first, stop=last), reads=[vsb[nv - 1], p_], writes=[oB])
            if mode == "diff":
                for half, eng in ((0, "dve"), (1, "pool")):
                    ac = pacc[qt % 2][half]
                    ph = p_[:, 512 * half:512 * half + 512]
                    if first:
                        sch.op(eng, lambda e, ac=ac, ph=ph: e.tensor_copy(out=ac[:, :], in_=ph), reads=[p_], writes=[ac])
                    else:
                        sch.op(eng, lambda e, ac=ac, ph=ph: e.tensor_tensor(out=ac[:, :], in0=ac[:, :], in1=ph, op=ALU.add),
                               reads=[p_, ac], writes=[ac])

        def emit_final(qt):
            t0 = qt * 512
            oA, oB, sA, sB = accs(qt)
            y_ = ysb[qt % 2]
            if mode != "diff":
                for r, oo in enumerate((oA, oB)):
                    rc_ = rcs[r]
                    sch.op("dve", lambda e, oo=oo, rc_=rc_: e.reciprocal(out=rc_[0:64, :], in_=oo[64:128, :]), reads=[oo], writes=[rc_])
                    sch.op("dve", lambda e, oo=oo, r=r, y_=y_, rc_=rc_: e.tensor_tensor(
                        out=y_[64 * r:64 * r + 64, :], in0=oo[0:64, :], in1=rc_[0:64, :], op=ALU.mult),
                        reads=[oo, rc_], writes=[y_])
                sch.dma(X["yT"][grp, :, t0:t0 + 512], y_[:, :], y_, reads=[y_], queue="pool")
            else:
                for half, sX in ((0, sA), (1, sB)):
                    ac = pacc[qt % 2][half]
                    sch.op("pe", lambda e, sX=sX, ac=ac: e.matmul(sX[:, :], lhsT=onesf[:, :], rhs=ac[:, :], start=True, stop=True),
                           reads=[onesf, ac], writes=[sX])
                sch.op("dve", lambda e: e.reciprocal(out=rc[:, :], in_=sA[:, :]), reads=[sA], writes=[rc])
                sch.op("dve", lambda e: e.tensor_tensor(out=y1[:, :], in0=oA[:, :], in1=rc[:, :], op=ALU.mult),
                       reads=[oA, rc], writes=[y1])
                sch.op("dve", lambda e: e.reciprocal(out=rc[:, :], in_=sB[:, :]), reads=[sB], writes=[rc])
                sch.op("dve", lambda e: e.tensor_tensor(out=y2[:, :], in0=oB[:, :], in1=rc[:, :], op=ALU.mult),
                       reads=[oB, rc], writes=[y2])
                sch.op("dve", lambda e: e.scalar_tensor_tensor(out=y1[:, :], in0=y2[:, :], scalar=nlam[:, 0:1], in1=y1[:, :],
                                                               op0=ALU.mult, op1=ALU.add), reads=[y1, y2, nlam], writes=[y1])
                sch.op("act", lambda e: e.activation(out=ysq[:, :], in_=y1[:, :], func=AF.Square), reads=[y1], writes=[ysq])
                sch.op("pe", lambda e: e.matmul(sA[:, :], lhsT=onesk[:, :], rhs=ysq[:, :], start=True, stop=True),
                       reads=[onesk, ysq], writes=[sA])
                sch.op("act", lambda e: e.activation(out=y2[:, :], in_=sA[:, :], func=AF.Ln, scale=1.0 / 128,
                                                     bias=c.epsb[:, 0:1]), reads=[sA, c.epsb], writes=[y2])
                sch.op("act", lambda e: e.activation(out=y2[:, :], in_=y2[:, :], func=AF.Exp, scale=-0.5),
                       reads=[y2], writes=[y2])
                dg = dgain[:, grp:grp + 1]
                sch.op("dve", lambda e, y_=y_, dg=dg: e.scalar_tensor_tensor(
                    out=y_[:, :], in0=y1[:, :], scalar=dg, in1=y2[:, :], op0=ALU.mult, op1=ALU.mult),
                    reads=[y1, y2, dgain], writes=[y_])
                sch.dma(X["yT"][4 + grp, :, t0:t0 + 512], y_[:, :], y_, reads=[y_], queue="pool")

        step0 = step
        load_q(0)
        emit_qk(0)
        for i in range(len(steps)):
            if i + 1 < len(steps):
                emit_qk(i + 1)
            emit_exp_pv(i)
            if steps[i][3]:
                emit_final(steps[i][0])
        step += len(steps)


@phase("gqa")
def p_gqa(c):
    p_attn(c, "gqa")


@phase("na")
def p_na(c):
    p_attn(c, "na")


@phase("diff")
def p_diff(c):
    p_attn(c, "diff")


@phase("mgate")
def p_mgate(c):
    sch, ar, I, X, S = c.sch, c.ar, c.I, c.X, c.S
    NB = c.NB
    load_consts(c)
    setup_small(c)
    oneb = ar.alloc("oneb", [128, 1])
    sch.op("dve", lambda e: e.memset(oneb[:, :], 1.0), writes=[oneb])
    A = ar.alloc("A", [36, S])
    B = ar.alloc("B", [36, S])
    Cc = ar.alloc("C", [36, S])
    Dd = ar.alloc("D", [36, S])
    for t_ in (A, B, Cc, Dd):
        sch.op("pool", lambda e, t_=t_: e.memset(t_[:, :], 0.0), writes=[t_])
    G = X["gates"]
    sch.dma(A[0:4, :], G[0:4, :], A, writes=[A])
    sch.dma(A[32:36, :], G[8:12, :], A, writes=[A])
    sch.dma(B[0:4, :], G[4:8, :], B, writes=[B])
    sch.dma(B[32:36, :], G[12:16, :], B, writes=[B])
    sch.op("dve", lambda e: e.tensor_copy(out=Cc[0:4, :], in_=A[0:4, :]), reads=[A], writes=[Cc])
    sch.op("dve", lambda e: e.tensor_copy(out=Cc[32:36, :], in_=A[32:36, ::-1]), reads=[A], writes=[Cc])
    sch.op("act", lambda e: e.activation(out=B[:, :], in_=B[:, :], func=AF.Exp, scale=-1.0), reads=[B], writes=[B])
    sch.op("act", lambda e: e.activation(out=B[:, :], in_=B[:, :], func=AF.Ln, bias=oneb[0:36, 0:1]), reads=[B, oneb], writes=[B])
    sch.op("dve", lambda e: e.tensor_copy(out=Dd[0:4, :], in_=B[0:4, :]), reads=[B], writes=[Dd])
    sch.op("dve", lambda e: e.tensor_copy(out=Dd[32:36, :], in_=B[32:36, ::-1]), reads=[B], writes=[Dd])
    sch.op("dve", lambda e: e.tensor_tensor_scan(out=A[:, :], data0=Dd[:, :], data1=Dd[:, :], initial=0.0,
                                                 op0=ALU.add, op1=ALU.max), reads=[Dd], writes=[A])
    sch.op("dve", lambda e: e.tensor_tensor(out=B[:, :], in0=Cc[:, :], in1=A[:, :], op=ALU.add), reads=[Cc, A], writes=[B])
    sch.op("dve", lambda e: e.tensor_tensor_scan(out=Cc[:, :], data0=B[:, :], data1=B[:, :], initial=0.0,
                                                 op0=ALU.max, op1=ALU.max), reads=[B], writes=[Cc])
    sch.op("dve", lambda e: e.tensor_tensor(out=Dd[:, :], in0=Cc[:, :], in1=A[:, :], op=ALU.subtract), reads=[Cc, A], writes=[Dd])
    sch.op("act", lambda e: e.activation(out=Dd[:, :], in_=Dd[:, :], func=AF.Exp, scale=-1.0), reads=[Dd], writes=[Dd])
    sch.op("dve", lambda e: e.tensor_scalar(out=A[:, :], in0=Cc[:, :], scalar1=-1.0, scalar2=None, op0=ALU.mult),
           reads=[Cc], writes=[A])
    sch.op("dve", lambda e: e.tensor_copy(out=Cc[0:4, :], in_=A[0:4, :]), reads=[A], writes=[Cc])
    sch.op("dve", lambda e: e.tensor_copy(out=Cc[32:36, :], in_=A[32:36, ::-1]), reads=[A], writes=[Cc])
    sch.op("dve", lambda e: e.tensor_copy(out=A[0:4, :], in_=Dd[0:4, :]), reads=[Dd], writes=[A])
    sch.op("dve", lambda e: e.tensor_copy(out=A[32:36, :], in_=Dd[32:36, ::-1]), reads=[Dd], writes=[A])
    sch.op("dve", lambda e: e.tensor_copy(out=Dd[0:4, :], in_=B[0:4, :]), reads=[B], writes=[Dd])
    sch.op("dve", lambda e: e.tensor_copy(out=Dd[32:36, :], in_=B[32:36, ::-1]), reads=[B], writes=[Dd])
    for d in range(2):
        sch.dma(X["grow"][4 * d:4 * d + 4, 0, :], Cc[32 * d:32 * d + 4, :], Cc, reads=[Cc], queue="pool")
        sch.dma(X["grow"][4 * d:4 * d + 4, 1, :], A[32 * d:32 * d + 4, :], A, reads=[A], queue="pool")
    ust = [ar.alloc("ust", [128, 4, 36]) for _ in range(2)]
    for kb4 in range(NB // 4):
        pst = c.ps[kb4 % 2]
        u_ = ust[kb4 % 2]
        for j in range(4):
            kb = kb4 * 4 + j
            sch.op("pe", lambda e, pst=pst, j=j, kb=kb: e.transpose(out=pst[:, j * 36:(j + 1) * 36],
                                                                  in_=Dd[0:36, kb * 128:(kb + 1) * 128],
                                                                  identity=c.idf[0:36, 0:36]),
                   reads=[Dd, c.idf], writes=[pst])
        sch.op("act", lambda e, pst=pst, u_=u_: e.activation(out=u_[:, :, :], in_=pst[:, 0:144].rearrange("p (j d) -> p j d", j=4),
                                                          func=AF.Copy), reads=[pst], writes=[u_])
        sch.dma(X["gv"][kb4 * 512:(kb4 + 1) * 512, 0:36].rearrange("(j p) d -> p j d", p=128), u_[:, :, :], u_,
                reads=[u_], queue="pool")


@phase("mlstm")
def p_mlstm(c):
    sch, ar, I, X, S = c.sch, c.ar, c.I, c.X, c.S
    NB, NT = c.NB, c.NT
    ps = c.ps
    load_consts(c)
    setup_small(c)
    oneb = ar.alloc("oneb", [128, 1])
    sch.op("dve", lambda e: e.memset(oneb[:, :], 1.0), writes=[oneb])
    onesk = ar.alloc("onesk", [128, 128], BF16)
    sch.op("dve", lambda e: e.memset(onesk[:, :], 1.0), writes=[onesk])
    onesf = ar.alloc("onesf", [128, 128])
    sch.op("dve", lambda e: e.memset(onesf[:, :], 1.0), writes=[onesf])
    mk = ar.alloc("mk", [128, 8, 512])
    sch.dma(mk[:, :, :], I["c_mmask"].rearrange("o p j -> p o j"), mk, writes=[mk])
    uT = ar.alloc("uT", [128, NB, 36])
    sch.dma(uT[:, :, :], X["gv"][:, 0:36].rearrange("(n p) d -> p n d", p=128), uT, writes=[uT])
    mg = ar.alloc("mg", [128, 4])
    sch.dma(mg[:, :], I["ev_mlstm_gain"].rearrange("(h p) -> p h", p=128), mg, writes=[mg], slow=True)
    kTs = ar.alloc("kTs", [128, S], BF16)
    vsb = ar.alloc("vsb", [128, NB, 128], BF16)
    qs = [ar.alloc("qs", [128, 512], BF16) for _ in range(3)]
    oms = [ar.alloc("oms", [128, 512], BF16) for _ in range(3)]
    rows = [[ar.alloc("rows", [128, 2, 512]) for _ in range(2)] for _ in range(3)]
    wt = [ar.alloc("wt", [128, 512]) for _ in range(4)]
    t1s = [ar.alloc("t1s", [128, 512]) for _ in range(2)]
    tmpm = [ar.alloc("tmpm", [128, 512]) for _ in range(3)]
    sc = [ar.alloc("sc", [128, 512], BF16) for _ in range(4)]
    dacc = [[ar.alloc("dacc", [128, 512]) for _ in range(2)] for _ in range(2)]
    stb = [ps[0], ps[1], ps[7]]
    hd = [ar.alloc("hd", [128, 512]) for _ in range(2)]
    t1 = ar.alloc("t1", [128, 512])
    t2 = ar.alloc("t2", [128, 512])
    ysb = [ar.alloc("ysb", [128, 512], BF16) for _ in range(2)]
    step = 0
    for h in range(4):
        sch.dma(kTs[:, :], X["kmT"][h], kTs, writes=[kTs])
        sch.dma(vsb[:, :, :], X["vm"][:, h, 0:128].rearrange("(n p) d -> p n d", p=128), vsb, writes=[vsb])
        steps = []
        for qt in range(NT):
            for d in range(2):
                kbs = list(range(0, 4 * qt + 4)) if d == 0 else list(range(4 * qt, NB))
                for ki, kb in enumerate(kbs):
                    steps.append((qt, d, kb, ki == 0, ki == len(kbs) - 1))

        def load_tile(qt):
            t0 = qt * 512
            sch.dma(qs[qt % 3][:, :], X["qmT"][h, :, t0:t0 + 512], qs[qt % 3], writes=[qs[qt % 3]])
            sch.dma(oms[qt % 3][:, :], X["omT"][h, :, t0:t0 + 512], oms[qt % 3], writes=[oms[qt % 3]])
            for d in range(2):
                r_ = rows[qt % 3][d]
                sch.dma(r_[:, :, :], X["grow"][4 * d + h, :, t0:t0 + 512].partition_broadcast(128), r_, writes=[r_])

        def emit_qk(i):
            qt, d, kb, first, last = steps[i]
            if first and d == 0 and qt + 1 < NT:
                load_tile(qt + 1)
            pa = stb[(step0 + i) % 3]
            q_ = qs[qt % 3]
            ks = slice(kb * 128, (kb + 1) * 128)
            sch.op("pe", lambda e, pa=pa, q_=q_, ks=ks: e.matmul(pa[:, :], lhsT=kTs[:, ks], rhs=q_[:, :], start=True, stop=True),
                   reads=[kTs, q_], writes=[pa])

        def emit_rest(i):
            qt, d, kb, first, last = steps[i]
            pa = stb[(step0 + i) % 3]
            w_ = wt[(step0 + i) % 4]
            s_ = sc[(step0 + i) % 4]
            tm = tmpm[(step0 + i) % 3]
            r_ = rows[qt % 3][d]
            oD, dD = ps[2 + 2 * d], ps[3 + 2 * d]
            ub = uT[:, kb, 32 * d + h:32 * d + h + 1]
            o = kb - 4 * qt
            if 0 <= o <= 3:
                mko = mk[:, 4 * d + o, :]
                sch.op("pool", lambda e, tm=tm, r_=r_, mko=mko: e.tensor_tensor(out=tm[:, :], in0=r_[:, 0, :], in1=mko, op=ALU.add),
                       reads=[r_, mk], writes=[tm])
                sch.op("act", lambda e, w_=w_, tm=tm, ub=ub: e.activation(out=w_[:, :], in_=tm[:, :], func=AF.Exp, bias=ub),
                       reads=[tm, uT], writes=[w_])
            else:
                sch.op("act", lambda e, w_=w_, r_=r_, ub=ub: e.activation(out=w_[:, :], in_=r_[:, 0, :], func=AF.Exp, bias=ub),
                       reads=[r_, uT], writes=[w_])
            sch.op("dve", lambda e, s_=s_, pa=pa, w_=w_: e.tensor_tensor(out=s_[:, :], in0=pa[:, :], in1=w_[:, :], op=ALU.mult),
                   reads=[pa, w_], writes=[s_])
            sch.op("pe", lambda e, oD=oD, kb=kb, s_=s_, first=first, last=last: e.matmul(
                oD[:, :], lhsT=vsb[:, kb, :], rhs=s_[:, :], start=first, stop=last), reads=[vsb, s_], writes=[oD])
            ac = dacc[qt % 2][d]
            if first:
                sch.op("pool", lambda e, ac=ac, s_=s_: e.tensor_copy(out=ac[:, :], in_=s_[:, :]), reads=[s_], writes=[ac])
            else:
                sch.op("pool", lambda e, ac=ac, s_=s_: e.tensor_tensor(out=ac[:, :], in0=ac[:, :], in1=s_[:, :], op=ALU.add),
                       reads=[s_, ac], writes=[ac])

        def emit_final_dir(qt, d):
            r_ = rows[qt % 3][d]
            oD, dD = ps[2 + 2 * d], ps[3 + 2 * d]
            hh = hd[d]
            t1_ = t1s[d]
            ac = dacc[qt % 2][d]
            sch.op("pe", lambda e, dD=dD, ac=ac: e.matmul(dD[:, :], lhsT=onesf[:, :], rhs=ac[:, :], start=True, stop=True),
                   reads=[onesf, ac], writes=[dD])
            sch.op("act", lambda e, dD=dD, t1_=t1_: e.activation(out=t1_[:, :], in_=dD[:, :], func=AF.Abs), reads=[dD], writes=[t1_])
            sch.op("dve", lambda e, r_=r_, t1_=t1_: e.tensor_tensor(out=t1_[:, :], in0=t1_[:, :], in1=r_[:, 1, :], op=ALU.max),
                   reads=[t1_, r_], writes=[t1_])
            sch.op("dve", lambda e, t1_=t1_: e.reciprocal(out=t1_[:, :], in_=t1_[:, :]), reads=[t1_], writes=[t1_])
            sch.op("dve", lambda e, oD=oD, hh=hh, t1_=t1_: e.tensor_tensor(out=hh[:, :], in0=oD[:, :], in1=t1_[:, :], op=ALU.mult),
                   reads=[oD, t1_], writes=[hh])

        def emit_combine(qt):
            t0 = qt * 512
            om_ = oms[qt % 3]
            y_ = ysb[qt % 2]
            sch.op("pool", lambda e: e.tensor_tensor(out=hd[0][:, :], in0=hd[0][:, :], in1=hd[1][:, :], op=ALU.add),
                   reads=[hd[0], hd[1]], writes=[hd[0]])
            sch.op("act", lambda e: e.activation(out=t2[:, :], in_=hd[0][:, :], func=AF.Square), reads=[hd[0]], writes=[t2])
            sch.op("pe", lambda e: e.matmul(ps[6][:, :], lhsT=onesf[:, :], rhs=t2[:, :], start=True, stop=True),
                   reads=[onesf, t2], writes=[ps[6]])
            sch.op("act", lambda e: e.activation(out=t2[:, :], in_=ps[6][:, :], func=AF.Ln, scale=1.0 / 128, bias=c.epsb[:, 0:1]),
                   reads=[ps[6], c.epsb], writes=[t2])
            sch.op("act", lambda e: e.activation(out=t2[:, :], in_=t2[:, :], func=AF.Exp, scale=-0.5), reads=[t2], writes=[t2])
            mgh = mg[:, h:h + 1]
            sch.op("dve", lambda e, mgh=mgh: e.scalar_tensor_tensor(out=hd[0][:, :], in0=hd[0][:, :], scalar=mgh, in1=t2[:, :],
                                                                     op0=ALU.mult, op1=ALU.mult), reads=[hd[0], t2, mg], writes=[hd[0]])
            sch.op("act", lambda e, om_=om_: e.activation(out=t2[:, :], in_=om_[:, :], func=AF.Exp, scale=-1.0), reads=[om_], writes=[t2])
            sch.op("pool", lambda e: e.tensor_scalar(out=t2[:, :], in0=t2[:, :], scalar1=1.0, scalar2=None, op0=ALU.add),
                   reads=[t2], writes=[t2])
            sch.op("dve", lambda e: e.reciprocal(out=t2[:, :], in_=t2[:, :]), reads=[t2], writes=[t2])
            sch.op("pool", lambda e, y_=y_: e.tensor_tensor(out=y_[:, :], in0=hd[0][:, :], in1=t2[:, :], op=ALU.mult),
                   reads=[hd[0], t2], writes=[y_])
            sch.dma(X["yT"][4 + h, :, t0:t0 + 512], y_[:, :], y_, reads=[y_], queue="pool")

        step0 = step
        load_tile(0)
        emit_qk(0)
        emit_qk(1)
        for i in range(len(steps)):
            if i + 2 < len(steps):
                emit_qk(i + 2)
            emit_rest(i)
            qt, d, kb, first, last = steps[i]
            if last:
                emit_final_dir(qt, d)
                if d == 1:
                    emit_combine(qt)
        step += len(steps)


def p_wout(c, layer, src, dst):
    sch, ar, I, X, S = c.sch, c.ar, c.I, c.X, c.S
    ps = c.ps
    wsb = load_weight(c, "ev_w_out" if layer == 0 else "od_w_out", D, D)
    ys = [ar.alloc("ys", [128, 8, 512], BF16) for _ in range(2)]
    xt = [ar.alloc("xt", [128, D]) for _ in range(3)]
    n = 0
    for ti in range(c.NT):
        t0 = ti * 512
        y_ = ys[ti % 2]
        sch.dma(y_[:, :, :], X["yT"][:, :, t0:t0 + 512].rearrange("k p t -> p k t"), y_, writes=[y_])
        for j in range(4):
            r0 = t0 + j * 128
            x_ = xt[n % 3]
            n += 1
            sch.dma(x_[:, :], src[r0:r0 + 128, :], x_, writes=[x_])
            for half in range(2):
                pst = ps[(2 * j + half) % 4]
                for k in range(8):
                    sch.op("pe", lambda e, pst=pst, y_=y_, k=k, j=j, half=half: e.matmul(
                        pst[:, :], lhsT=y_[:, k, j * 128:(j + 1) * 128], rhs=wsb[:, k, half * 512:(half + 1) * 512],
                        start=(k == 0), stop=(k == 7)), reads=[y_, wsb], writes=[pst])
                sch.op("dve", lambda e, pst=pst, x_=x_, half=half: e.tensor_tensor(
                    out=x_[:, half * 512:(half + 1) * 512], in0=pst[:, :], in1=x_[:, half * 512:(half + 1) * 512], op=ALU.add),
                    reads=[pst, x_], writes=[x_])
            sch.dma(dst[r0:r0 + 128, :], x_[:, :], x_, reads=[x_], queue="pool")


def p_cross(c, layer, src, dst):
    sch, ar, I, X, S = c.sch, c.ar, c.I, c.X, c.S
    ps = c.ps
    load_consts(c)
    setup_small(c, ln_consts=(math.log(128.0 ** -0.5),))
    wq = load_weight(c, "ca_w_q%d" % layer, D, 512, tag="wq")
    wo = load_weight(c, "ca_w_o%d" % layer, 512, D, tag="wo")
    wkv = load_weight(c, "ca_w_kv%d" % layer, D, 1024, tag="wkv")
    g = ar.alloc("g", [128, 2, 128])
    for i in range(2):
        sch.dma(g[:, i, :], I["ca_qk_gain"][layer, i].partition_broadcast(128), g, writes=[g])
    onesk = ar.alloc("onesk", [128, 128], BF16)
    sch.op("dve", lambda e: e.memset(onesk[:, :], 1.0), writes=[onesk])
    xt = [ar.alloc("xt", [128, D]) for _ in range(8)]
    xn = [ar.alloc("xn", [128, D], BF16) for _ in range(2)]
    ss = [ar.alloc("ss", [128, 1]) for _ in range(2)]
    rs = [ar.alloc("rs", [128, 1]) for _ in range(2)]
    hT = [ar.alloc("hT", [128, 8, 512], BF16) for _ in range(2)]
    sq = ar.alloc("sq", [128, 512])
    ssh = ar.alloc("ssh", [128, 4])
    rq = ar.alloc("rq", [128, 4])
    t1 = ar.alloc("t1", [128, 512])
    qn = [ar.alloc("qn", [128, 512], BF16) for _ in range(2)]
    KT = ar.alloc("KT", [128, 4, MEM], BF16)
    V = ar.alloc("V", [128, 2, 512], BF16)
    QT = [ar.alloc("QT", [128, 4, 512], BF16) for _ in range(2)]
    OT = [ar.alloc("OT", [128, 4, 512], BF16) for _ in range(2)]
    pT = [ar.alloc("pT", [128, 512], BF16) for _ in range(3)]
    rc = ar.alloc("rc", [128, 512])

    def headnorm_T(pst, gi, lnscale, stage, col0, qi):
        sch.op("act", lambda e: e.activation(out=sq[:, :], in_=pst[:, :], func=AF.Square), reads=[pst], writes=[sq])
        sch.op("dve", lambda e: e.tensor_reduce(out=ssh[:, :], in_=sq[:, :].rearrange("p (h d) -> p h d", h=4), axis=AX.X,
                                                op=ALU.add), reads=[sq], writes=[ssh])
        rms_rstd(c, ssh, rq, 128, 4, extra_ln=lnscale)
        q_ = qn[qi % 2]
        sch.op("dve", lambda e: e.tensor_tensor(out=t1[:, :].rearrange("p (h d) -> p h d", h=4),
                                                in0=pst[:, :].rearrange("p (h d) -> p h d", h=4),
                                                in1=g[:, gi:gi + 1, :].to_broadcast([128, 4, 128]), op=ALU.mult),
               reads=[pst, g], writes=[t1])
        sch.op("dve", lambda e: e.tensor_tensor(out=q_[:, :].rearrange("p (h d) -> p h d", h=4),
                                                in0=t1[:, :].rearrange("p (h d) -> p h d", h=4),
                                                in1=rq[:, 0:4].unsqueeze(2).to_broadcast([128, 4, 128]), op=ALU.mult),
               reads=[t1, rq], writes=[q_])
        pt = ps[6 + qi % 2]
        pv = pt.ap.bitcast(BF16)
        for h in range(4):
            sch.op("pe", lambda e, h=h: e.transpose(out=pv[:, h * 128:(h + 1) * 128], in_=q_[:, h * 128:(h + 1) * 128],
                                                    identity=c.idb[:, :]), reads=[q_, c.idb], writes=[pt])
        sch.op("act", lambda e: e.activation(out=stage[:, :, col0:col0 + 128],
                                             in_=pv[:, 0:512].rearrange("p (h t) -> p h t", h=4), func=AF.Copy),
               reads=[pt], writes=[stage])

    mT = hT[1]
    for b in range(2):
        norm_transpose(c, I["mem"], b * 128, 128, mT, b * 128, xt[b], xn[b], ss[b], rs[b], ps[b])
    for b in range(2):
        pk, pvv = ps[2 + b], ps[4 + b]
        for k in range(8):
            sch.op("pe", lambda e, k=k, pk=pk, b=b: e.matmul(pk[:, :], lhsT=mT[:, k, b * 128:(b + 1) * 128], rhs=wkv[:, k, 0:512],
                                                             start=(k == 0), stop=(k == 7)), reads=[mT, wkv], writes=[pk])
        for k in range(8):
            sch.op("pe", lambda e, k=k, pvv=pvv, b=b: e.matmul(pvv[:, :], lhsT=mT[:, k, b * 128:(b + 1) * 128], rhs=wkv[:, k, 512:1024],
                                                               start=(k == 0), stop=(k == 7)), reads=[mT, wkv], writes=[pvv])
        headnorm_T(pk, 1, 0.0, KT, b * 128, b)
        sch.op("act", lambda e, pvv=pvv, b=b: e.activation(out=V[:, b, :], in_=pvv[:, :], func=AF.Copy), reads=[pvv], writes=[V])
    step = 0
    qi = 0
    for ti in range(c.NT):
        t0 = ti * 512
        h_ = hT[0]
        xs_ = [xt[4 * (ti % 2) + j] for j in range(4)]
        for j in range(4):
            norm_transpose(c, src, t0 + j * 128, 128, h_, j * 128, xs_[j], xn[j % 2], ss[j % 2], rs[j % 2], ps[j % 2])
        qT_, oT_ = QT[ti % 2], OT[ti % 2]
        for j in range(4):
            pst = ps[2 + j % 2]
            for k in range(8):
                sch.op("pe", lambda e, k=k, pst=pst, j=j: e.matmul(pst[:, :], lhsT=h_[:, k, j * 128:(j + 1) * 128], rhs=wq[:, k, :],
                                                                   start=(k == 0), stop=(k == 7)), reads=[h_, wq], writes=[pst])
            headnorm_T(pst, 0, math.log(128.0 ** -0.5), qT_, j * 128, qi)
            qi += 1
        for h in range(4):
            oD, sD = ps[4], ps[5]
            for kb in range(2):
                pa = ps[step % 2]
                p_ = pT[step % 3]
                step += 1
                sch.op("pe", lambda e, pa=pa, h=h, kb=kb, qT_=qT_: e.matmul(pa[:, :], lhsT=KT[:, h, kb * 128:(kb + 1) * 128],
                                                                           rhs=qT_[:, h, :], start=True, stop=True),
                       reads=[KT, qT_], writes=[pa])
                sch.op("act", lambda e, pa=pa, p_=p_: e.activation(out=p_[:, :], in_=pa[:, :], func=AF.Exp), reads=[pa], writes=[p_])
                sch.op("pe", lambda e, oD=oD, h=h, kb=kb, p_=p_: e.matmul(oD[:, :], lhsT=V[:, kb, h * 128:(h + 1) * 128], rhs=p_[:, :],
                                                                         start=(kb == 0), stop=(kb == 1)), reads=[V, p_], writes=[oD])
                sch.op("pe", lambda e, sD=sD, p_=p_, kb=kb: e.matmul(sD[:, :], lhsT=onesk[:, :], rhs=p_[:, :],
                                                                    start=(kb == 0), stop=(kb == 1)), reads=[onesk, p_], writes=[sD])
            sch.op("dve", lambda e, sD=sD: e.reciprocal(out=rc[:, :], in_=sD[:, :]), reads=[sD], writes=[rc])
            sch.op("dve", lambda e, oD=oD, oT_=oT_, h=h: e.tensor_tensor(out=oT_[:, h, :], in0=oD[:, :], in1=rc[:, :], op=ALU.mult),
                   reads=[oD, rc], writes=[oT_])
        for j in range(4):
            r0 = t0 + j * 128
            x_ = xs_[j]
            for half in range(2):
                pst = ps[2 + half]
                for h in range(4):
                    sch.op("pe", lambda e, pst=pst, oT_=oT_, h=h, j=j, half=half: e.matmul(
                        pst[:, :], lhsT=oT_[:, h, j * 128:(j + 1) * 128], rhs=wo[:, h, half * 512:(half + 1) * 512],
                        start=(h == 0), stop=(h == 3)), reads=[oT_, wo], writes=[pst])
                sch.op("dve", lambda e, pst=pst, x_=x_, half=half: e.tensor_tensor(
                    out=x_[:, half * 512:(half + 1) * 512], in0=pst[:, :], in1=x_[:, half * 512:(half + 1) * 512], op=ALU.add),
                    reads=[pst, x_], writes=[x_])
            sch.dma(dst[r0:r0 + 128, :], x_[:, :], x_, reads=[x_], queue="pool")


def p_ffn(c, layer, src, dst):
    sch, ar, I, X, S = c.sch, c.ar, c.I, c.X, c.S
    ps = c.ps
    NCH = D_FF // 128
    load_consts(c)
    setup_small(c)
    wup = load_weight(c, "ffn_w_up%d" % layer, D, 2 * D_FF, tag="wup")
    wdn = load_weight(c, "ffn_w_down%d" % layer, D_FF, D, tag="wdn")
    cw = ar.alloc("cw", [128, 3, 2 * NCH])
    cb = ar.alloc("cb", [128, 2 * NCH])
    for j in range(3):
        sch.dma(cw[:, j, :], I["ffn_conv_w"][layer, j].rearrange("(c p) -> p c", p=128), cw, writes=[cw], slow=True)
    sch.dma(cb[:, :], I["ffn_conv_b"][layer].rearrange("(c p) -> p c", p=128), cb, writes=[cb], slow=True)
    xt = [ar.alloc("xt", [128, D]) for _ in range(4)]
    xh = ar.alloc("xh", [128, D])
    xn = [ar.alloc("xn", [128, D], BF16) for _ in range(2)]
    ss = [ar.alloc("ss", [128, 1]) for _ in range(2)]
    rs = [ar.alloc("rs", [128, 1]) for _ in range(2)]
    hT = ar.alloc("hT", [128, 8, 514], BF16)
    G = ar.alloc("G", [128, NCH, 512], BF16)
    cg = [ar.alloc("cg", [128, 512]) for _ in range(2)]
    cv = [ar.alloc("cv", [128, 512]) for _ in range(2)]
    sg = [ar.alloc("sg", [128, 512]) for _ in range(2)]
    n = 0
    for ti in range(c.NT):
        t0 = ti * 512
        for j in range(4):
            norm_transpose(c, src, t0 + j * 128, 128, hT, 1 + j * 128, xt[j], xn[j % 2], ss[j % 2], rs[j % 2], ps[j % 2])
        sch.op("pool", lambda e: e.memset(xh[:, :], 0.0), writes=[xh])
        if t0 > 0:
            sch.dma(xh[0:1, :], src[t0 - 1:t0, :], xh, writes=[xh])
        if t0 + 512 < S:
            sch.dma(xh[1:2, :], src[t0 + 512:t0 + 513, :], xh, writes=[xh])
        sch.op("act", lambda e: e.activation(out=xn[0][:, :], in_=xh[:, :], func=AF.Square, accum_out=ss[0][:, 0:1]),
               reads=[xh], writes=[xn[0], ss[0]])
        rms_rstd(c, ss[0], rs[0], D, 1)
        sch.op("dve", lambda e: e.tensor_scalar(out=xn[0][:, :], in0=xh[:, :], scalar1=rs[0][:, 0:1], scalar2=None, op0=ALU.mult),
               reads=[xh, rs[0]], writes=[xn[0]])
        pvh = ps[0].ap.bitcast(BF16)
        for k in range(8):
            sch.op("pe", lambda e, k=k: e.transpose(out=pvh[:, k * 128:(k + 1) * 128], in_=xn[0][:, k * 128:(k + 1) * 128],
                                                    identity=c.idb[:, :]), reads=[xn[0], c.idb], writes=[ps[0]])
        sch.op("act", lambda e: e.activation(out=hT[:, :, 0:1], in_=pvh.rearrange("p (k t) -> p k t", k=8)[:, :, 0:1], func=AF.Copy),
               reads=[ps[0]], writes=[hT])
        sch.op("act", lambda e: e.activation(out=hT[:, :, 513:514], in_=pvh.rearrange("p (k t) -> p k t", k=8)[:, :, 1:2], func=AF.Copy),
               reads=[ps[0]], writes=[hT])
        for f in range(NCH):
            outs = []
            for which in range(2):
                ch = f + which * NCH
                pp = 2 * ((2 * f + which) % 4)
                pa, pb = ps[pp], ps[pp + 1]
                for half, pst in enumerate((pa, pb)):
                    for k in range(8):
                        sch.op("pe", lambda e, k=k, pst=pst, ch=ch, half=half: e.matmul(
                            pst[:, 0:258], lhsT=wup[:, k, ch * 128:(ch + 1) * 128], rhs=hT[:, k, 256 * half:256 * half + 258],
                            start=(k == 0), stop=(k == 7)), reads=[hT, wup], writes=[pst])
                pv2 = c.psall[:, pp * 512:(pp + 2) * 512].rearrange("p (a b) -> p a b", a=2)
                dst_c = (cg if which == 0 else cv)[n % 2]
                dv = dst_c[:, :].rearrange("p (a b) -> p a b", a=2)
                sch.op("act", lambda e, dv=dv, pv2=pv2, ch=ch: e.activation(
                    out=dv, in_=pv2[:, :, 1:257], func=AF.Identity, scale=cw[:, 1, ch:ch + 1], bias=cb[:, ch:ch + 1]),
                    reads=[pa, pb, cw, cb], writes=[dst_c])
                for sh, wi in ((0, 0), (2, 2)):
                    sch.op("dve", lambda e, dv=dv, pv2=pv2, ch=ch, sh=sh, wi=wi: e.scalar_tensor_tensor(
                        out=dv, in0=pv2[:, :, sh:sh + 256], scalar=cw[:, wi, ch:ch + 1], in1=dv, op0=ALU.mult, op1=ALU.add),
                        reads=[pa, pb, cw, dst_c], writes=[dst_c])
                outs.append(dst_c)
            s_ = sg[n % 2]
            sch.op("act", lambda e, s_=s_, a=outs[0]: e.activation(out=s_[:, :], in_=a[:, :], func=AF.Silu), reads=[outs[0]], writes=[s_])
            sch.op("dve", lambda e, s_=s_, b=outs[1], f=f: e.tensor_tensor(out=G[:, f, :], in0=s_[:, :], in1=b[:, :], op=ALU.mult),
                   reads=[s_, outs[1]], writes=[G])
            n += 1
        for j in range(4):
            r0 = t0 + j * 128
            x_ = xt[j]
            for half in range(2):
                pst = ps[(2 * j + half) % 8]
                for f in range(NCH):
                    sch.op("pe", lambda e, pst=pst, f=f, j=j, half=half: e.matmul(
                        pst[:, :], lhsT=G[:, f, j * 128:(j + 1) * 128], rhs=wdn[:, f, half * 512:(half + 1) * 512],
                        start=(f == 0), stop=(f == NCH - 1)), reads=[G, wdn], writes=[pst])
                sch.op("dve", lambda e, pst=pst, x_=x_, half=half: e.tensor_tensor(
                    out=x_[:, half * 512:(half + 1) * 512], in0=pst[:, :], in1=x_[:, half * 512:(half + 1) * 512], op=ALU.add),
                    reads=[pst, x_], writes=[x_])
            sch.dma(dst[r0:r0 + 128, :], x_[:, :], x_, reads=[x_], queue="pool")


@phase("wout0")
def _p(c):
    p_wout(c, 0, c.I["x"], c.X["x1"])


@phase("cross0")
def _p(c):
    p_cross(c, 0, c.X["x1"], c.X["x2"])


@phase("ffn0")
def _p(c):
    p_ffn(c, 0, c.X["x2"], c.X["x1"])


@phase("wout1")
def _p(c):
    p_wout(c, 1, c.X["x1"], c.X["x2"])


@phase("cross1")
def _p(c):
    p_cross(c, 1, c.X["x2"], c.X["x1"])


@phase("ffn1")
def _p(c):
    p_ffn(c, 1, c.X["x1"], c.out)


def t5_bucket_np(rel):
    nb = 16
    max_exact = 8
    n = np.abs(rel)
    lr = np.log(np.maximum(n, 1).astype(np.float32) / max_exact) / math.log(128 / max_exact)
    large = np.minimum(max_exact + (lr * (nb - max_exact)).astype(np.int32), nb - 1)
    return np.where(rel > 0, nb, 0) + np.where(n < max_exact, n, large)


def host_consts(S, inputs):
    cst = {}
    cst["c_ident"] = np.eye(128, dtype=np.float32)
    pos = np.arange(S)
    rows, cols = pos // GRID_W, pos % GRID_W
    inv = (10000.0 ** (-np.arange(0, 32, 2, dtype=np.float32) / 32)).astype(np.float32)
    ar_ = rows.astype(np.float32)[:, None] * inv[None, :]
    ac_ = cols.astype(np.float32)[:, None] * inv[None, :]
    cst["c_cos"] = np.concatenate([np.cos(ar_), np.cos(ar_), np.cos(ac_), np.cos(ac_)], axis=1).astype(np.float32)
    cst["c_sin"] = np.concatenate([-np.sin(ar_), np.sin(ar_), -np.sin(ac_), np.sin(ac_)], axis=1).astype(np.float32)
    i = np.arange(128)
    cst["c_trif"] = (i[:, None] <= i[None, :]).astype(np.float32)
    cst["c_trib"] = (i[:, None] >= i[None, :]).astype(np.float32)
    t5 = np.asarray(inputs["t5_table"], np.float32)
    til = np.zeros((4, 6, 128, 512), np.float32)
    for o in range(6):
        k = (o - 1) * 128 + np.arange(128)
        q = np.arange(512)
        idx = t5_bucket_np(k[:, None] - q[None, :])
        for h in range(4):
            til[h, o] = t5[idx, h]
    cst["c_t5"] = til
    far = np.zeros((128, 8), np.float32)
    for h in range(4):
        far[:, h] = t5[15, h]
        far[:, 4 + h] = t5[31, h]
    cst["c_t5far"] = far
    rpb = np.asarray(inputs["od_na_rpb"], np.float32)[0]
    nrows = S // GRID_W
    nab = np.full((8, 3, 8, 128, 512), NEG, np.float32)
    ntile = S // 512
    for var, qt in enumerate([0, min(1, ntile - 1), ntile - 1]):
        for o in range(8):
            kb = 4 * qt - 2 + o
            if kb < 0 or kb >= S // 128:
                continue
            kpos = kb * 128 + np.arange(128)
            qpos = qt * 512 + np.arange(512)
            kr, kc = kpos // GRID_W, kpos % GRID_W
            qr, qc = qpos // GRID_W, qpos % GRID_W
            r0 = np.clip(qr - 4, 0, nrows - 8)
            c0 = np.clip(qc - 8, 0, GRID_W - 16)
            ok = ((kr[:, None] >= r0[None, :]) & (kr[:, None] < r0[None, :] + 8) &
                  (kc[:, None] >= c0[None, :]) & (kc[:, None] < c0[None, :] + 16))
            dr = np.clip(kr[:, None] - qr[None, :] + 7, 0, 14)
            dc = np.clip(kc[:, None] - qc[None, :] + 15, 0, 30)
            for h in range(8):
                nab[h, var, o] = np.where(ok, rpb[h][dr, dc], NEG)
    cst["c_nab"] = nab
    mm = np.zeros((8, 128, 512), np.float32)
    for o in range(4):
        sl = o * 128 + np.arange(128)[:, None]
        tl = np.arange(512)[None, :]
        mm[o] = np.where(sl <= tl, 0.0, NEG)
        mm[4 + o] = np.where(sl >= tl, 0.0, NEG)
    cst["c_mmask"] = mm
    return cst


IN_NAMES = ["norm_mix", "norm_cross", "norm_mem", "norm_ffn", "ev_w_in", "ev_gate_bias", "ev_attn_qk_gain",
            "ev_mlstm_gain", "ev_w_out", "od_w_in", "od_na_qk_gain", "od_diff_qk_gain", "od_diff_lambda",
            "od_diff_gain", "od_w_out", "ca_w_q", "ca_w_kv", "ca_qk_gain", "ca_w_o", "ffn_w_up", "ffn_conv_w",
            "ffn_conv_b", "ffn_w_down"]
SQUEEZE0 = {"ev_w_in", "ev_gate_bias", "ev_attn_qk_gain", "ev_mlstm_gain", "ev_w_out", "od_w_in", "od_na_qk_gain",
            "od_diff_qk_gain", "od_diff_lambda", "od_diff_gain", "od_w_out"}


def make_in_maps(inputs, ncores, S):
    cst = host_consts(S, inputs)
    shared = {}
    for nm in IN_NAMES:
        a = np.ascontiguousarray(np.asarray(inputs[nm], np.float32))
        if nm in SQUEEZE0:
            a = a[0]
        shared[nm] = np.ascontiguousarray(a)
    shared.update(cst)
    maps = []
    for i in range(ncores):
        m = dict(shared)
        m["x"] = np.ascontiguousarray(np.asarray(inputs["x"][i], np.float32))
        m["mem"] = np.ascontiguousarray(np.asarray(inputs["mem"][i], np.float32))
        maps.append(m)
    return maps


def kernel(**inputs):
    x = inputs["x"]
    B, S, _ = x.shape
    nc = build(S)
    maps = make_in_maps(inputs, B, S)
    res = run_bass_kernel_spmd(nc, maps, core_ids=list(range(B)))
    return np.stack([np.asarray(r["out"], np.float32) for r in res.results], axis=0)
```

```python
import math
import numpy as np
import concourse.bass as bass
import concourse.mybir as mybir
from concourse.bass_utils import run_bass_kernel_spmd

F32 = mybir.dt.float32
BF16 = mybir.dt.bfloat16
AF = mybir.ActivationFunctionType
ALU = mybir.AluOpType
AX = mybir.AxisListType

D = 1024
GRID_W = 64
EPS = 1e-6
D_FF = 2816
MEM = 256
NEG = -30000.0


class T:
    def __init__(self, name, ap):
        self.name = name
        self.ap = ap
        self.w = None
        self.r = {}
        self.slot = {}

    def __getitem__(self, k):
        return self.ap[k]


class SemSlot:
    def __init__(self, sem):
        self.sem = sem
        self.cnt = 0


class Sched:
    COMPUTE = ("pe", "act", "dve", "pool")

    def __init__(self, nc):
        self.nc = nc
        self.eng = {"pe": nc.tensor, "act": nc.scalar, "dve": nc.vector, "pool": nc.gpsimd,
                    "sp": nc.sync}
        self.streams = {k: [] for k in self.eng}
        self.esem = {k: nc.alloc_semaphore("es_" + k) for k in self.COMPUTE}
        self.cnt = {k: 0 for k in self.COMPUTE}
        self.waited = {k: {} for k in self.eng}
        self.latest = {}
        self.free_slots = {}
        self.nslots = 0
        self.phase_bufs = []

    def _slot(self, queue):
        fl = self.free_slots.setdefault(queue, [])
        if fl:
            return fl.pop()
        self.nslots += 1
        return SemSlot(self.nc.alloc_semaphore("ds%d" % self.nslots))

    @staticmethod
    def _key(tok):
        return (tok[0], tok[1])

    def _need(self, stream, deps):
        out = {}
        w = self.waited[stream]
        for tok in deps:
            k = (tok[0], tok[1])
            v = tok[2]
            if w.get(k, 0) >= v:
                continue
            if out.get(k, 0) < v:
                out[k] = v
        for k, v in out.items():
            w[k] = v
        return [(k[0], k[1], v) for k, v in out.items()]

    def _deps(self, reads, writes):
        deps = []
        for b in reads:
            if b.w is not None:
                deps.append(b.w)
        for b in writes:
            if b.w is not None:
                deps.append(b.w)
            for k, v in b.r.items():
                deps.append((k[0], k[1], v))
        return deps

    def _mark(self, tok, reads, writes):
        k = (tok[0], tok[1])
        self.latest[k] = tok[2]
        for b in reads:
            b.r[k] = tok[2]
        for b in writes:
            b.w = tok
            b.r = {}

    def op(self, eng, fn, reads=(), writes=()):
        deps = self._deps(reads, writes)
        if eng == "pe":
            deps = [d for d in deps if not (d[0] == "e" and d[1] == "pe")]
        waits = self._need(eng, deps)
        self.cnt[eng] += 1
        tok = ("e", eng, self.cnt[eng])
        self.streams[eng].append(["op", fn, waits, self.cnt[eng]])
        self._mark(tok, reads, writes)

    def dma(self, out_ap, in_ap, sb, reads=(), writes=(), queue="sp", slow=False):
        deps = self._deps(reads, writes)
        waits = self._need(queue, deps)
        if queue not in sb.slot:
            if not sb.slot:
                self.phase_bufs.append(sb)
            sb.slot[queue] = self._slot(queue)
        sl = sb.slot[queue]
        sl.cnt += 1
        tok = ("d", sl, 16 * sl.cnt)
        self.streams[queue].append(["dma", (out_ap, in_ap, slow), waits, sl.sem])
        self._mark(tok, reads, writes)

    def barrier(self):
        for s in self.streams:
            deps = [(k[0], k[1], v) for k, v in self.latest.items()]
            if s == "pe":
                pass
            waits = self._need(s, deps)
            if waits:
                self.streams[s].append(["wait", None, waits, None])
        for b in self.phase_bufs:
            for q, sl in b.slot.items():
                self.free_slots.setdefault(q, []).append(sl)
            b.slot = {}
        self.phase_bufs = []

    def emit(self):
        nc = self.nc
        needed = {k: set() for k in self.COMPUTE}
        for s, lst in self.streams.items():
            for ent in lst:
                for (kind, who, v) in ent[2]:
                    if kind == "e":
                        needed[who].add(v)
        semval = {}
        for k in self.COMPUTE:
            arr = sorted(needed[k])
            semval[k] = {v: i + 1 for i, v in enumerate(arr)}
        self.final = {k: len(needed[k]) for k in self.COMPUTE}

        def lower(wt):
            kind, who, v = wt
            if kind == "e":
                return self.esem[who], semval[who][v]
            return who.sem, v

        def run(sname, engine):
            for ent in self.streams[sname]:
                kind, payload, waits, extra = ent
                lw = [lower(w) for w in waits]
                if kind == "wait":
                    for (sem, val) in lw:
                        engine.wait_ge(sem, val)
                    continue
                for (sem, val) in lw[1:]:
                    engine.wait_ge(sem, val)
                if kind == "op":
                    ins = payload(engine)
                    if lw:
                        ins._wait_ge(lw[0][0], lw[0][1])
                    if extra in semval[sname]:
                        ins.then_inc(self.esem[sname], 1)
                else:
                    out_ap, in_ap, slow = payload
                    if slow:
                        ins = engine.dma_start(out=out_ap, in_=in_ap, allow_slow_non_contiguous=True)
                    else:
                        ins = engine.dma_start(out=out_ap, in_=in_ap)
                    if lw:
                        ins._wait_ge(lw[0][0], lw[0][1])
                    ins.then_inc(extra, 16)

        with nc.Block() as block:
            @block.sync
            def _(e):
                run("sp", e)

            @block.tensor
            def _(e):
                run("pe", e)

            @block.scalar
            def _(e):
                run("act", e)

            @block.vector
            def _(e):
                run("dve", e)

            @block.gpsimd
            def _(e):
                run("pool", e)


class Arena:
    def __init__(self, nc):
        rem = nc.sbuf_bytes_remaining
        self.nwords = (rem - 2048) // 4
        self.base = nc.alloc_sbuf_tensor("arena", [128, self.nwords], F32)
        self.top = 0
        self.n = 0

    def reset(self, keep=0):
        self.top = keep

    def alloc(self, name, shape, dtype=F32):
        free = 1
        for s in shape[1:]:
            free *= s
        words = free if dtype == F32 else (free + 1) // 2
        words = (words + 7) // 8 * 8
        if self.top + words > self.nwords:
            raise RuntimeError("SBUF arena overflow at %s: need %d words, top %d of %d"
                               % (name, words, self.top, self.nwords))
        ap = self.base[0:shape[0], self.top:self.top + words]
        self.top += words
        if dtype != F32:
            ap = ap.bitcast(dtype)
        ap = ap[:, 0:free]
        if len(shape) == 3:
            ap = ap.rearrange("p (a b) -> p a b", a=shape[1])
        elif len(shape) == 4:
            ap = ap.rearrange("p (a b c) -> p a b c", a=shape[1], b=shape[2])
        self.n += 1
        return T("%s_%d" % (name, self.n), ap)


class Ctx:
    pass


def dram_in(nc, name, shape, dtype=F32):
    return nc.dram_tensor(name, list(shape), dtype, kind="ExternalInput").ap()


DBG = [False]


def dram_tmp(nc, name, shape, dtype=BF16):
    kind = "ExternalOutput" if (DBG[0] and not name.startswith("wb_")) else "Internal"
    return nc.dram_tensor(name, list(shape), dtype, kind=kind).ap()


def build(S, phases=None, dbg=None):
    nc = bass.Bass("TRN2", target_bir_lowering=False)
    c = Ctx()
    c.nc = nc
    c.S = S
    c.NT = S // 512
    c.NB = S // 128
    sch = Sched(nc)
    c.sch = sch
    ar = Arena(nc)
    c.ar = ar
    c.psall = nc.alloc_psum_tensor("psall", [128, 4096], F32)
    c.ps = [T("ps%d" % i, c.psall[:, i * 512:(i + 1) * 512]) for i in range(8)]

    I = {}
    I["x"] = dram_in(nc, "x", [S, D])
    I["mem"] = dram_in(nc, "mem", [MEM, D])
    for nm, shp in [("norm_mix", [2, D]), ("norm_cross", [2, D]), ("norm_mem", [2, D]), ("norm_ffn", [2, D]),
                    ("ev_w_in", [D, 2832]), ("ev_gate_bias", [16]), ("ev_attn_qk_gain", [2, 64]),
                    ("ev_mlstm_gain", [512]), ("ev_w_out", [D, D]),
                    ("od_w_in", [D, 3072]), ("od_na_qk_gain", [2, 64]), ("od_diff_qk_gain", [2, 64]),
                    ("od_diff_lambda", [4, 64]), ("od_diff_gain", [512]), ("od_w_out", [D, D]),
                    ("ca_w_q", [2, D, 512]), ("ca_w_kv", [2, D, 1024]), ("ca_qk_gain", [2, 2, 128]),
                    ("ca_w_o", [2, 512, D]), ("ffn_w_up", [2, D, 2 * D_FF]), ("ffn_conv_w", [2, 3, 2 * D_FF]),
                    ("ffn_conv_b", [2, 2 * D_FF]), ("ffn_w_down", [2, D_FF, D]),
                    ("c_ident", [128, 128]), ("c_cos", [S, 64]), ("c_sin", [S, 64]),
                    ("c_trif", [128, 128]), ("c_trib", [128, 128]),
                    ("c_t5", [4, 6, 128, 512]), ("c_t5far", [128, 8]),
                    ("c_nab", [8, 3, 8, 128, 512]), ("c_mmask", [8, 128, 512])]:
        I[nm] = dram_in(nc, nm, shp)
    c.I = I
    c.out = nc.dram_tensor("out", [S, D], F32, kind="ExternalOutput").ap()

    W = {}
    for nm, shp in [("ev_w_in", [D, 2832]), ("ev_w_out", [D, D]), ("od_w_in", [D, 3072]), ("od_w_out", [D, D]),
                    ("ca_w_q0", [D, 512]), ("ca_w_q1", [D, 512]), ("ca_w_kv0", [D, 1024]), ("ca_w_kv1", [D, 1024]),
                    ("ca_w_o0", [512, D]), ("ca_w_o1", [512, D]),
                    ("ffn_w_up0", [D, 2 * D_FF]), ("ffn_w_up1", [D, 2 * D_FF]),
                    ("ffn_w_down0", [D_FF, D]), ("ffn_w_down1", [D_FF, D])]:
        W[nm] = dram_tmp(nc, "wb_" + nm, shp)
    c.W = W
    X = {}
    X["x1"] = dram_tmp(nc, "x1", [S, D], F32)
    X["x2"] = dram_tmp(nc, "x2", [S, D], F32)
    X["qT"] = dram_tmp(nc, "qT", [4, 128, S])
    X["kT"] = dram_tmp(nc, "kT", [4, 128, S])
    X["v"] = dram_tmp(nc, "v", [S, 8, 128])
    X["qT2"] = dram_tmp(nc, "qT2", [4, 128, S])
    X["kT2"] = dram_tmp(nc, "kT2", [4, 128, S])
    X["v2"] = dram_tmp(nc, "v2", [S, 4, 128])
    X["qmT"] = dram_tmp(nc, "qmT", [4, 128, S])
    X["kmT"] = dram_tmp(nc, "kmT", [4, 128, S])
    X["km"] = dram_tmp(nc, "km", [S, 512])
    X["vm"] = dram_tmp(nc, "vm", [S, 4, 130])
    X["omT"] = dram_tmp(nc, "omT", [4, 128, S])
    X["grow"] = dram_tmp(nc, "grow", [8, 2, S], F32)
    X["gates"] = dram_tmp(nc, "gates", [16, S], F32)
    X["gv"] = dram_tmp(nc, "gv", [S, 48], F32)
    X["gc"] = dram_tmp(nc, "gc", [S // 128, 8], F32)
    X["hf"] = dram_tmp(nc, "hf", [S, 512], F32)
    X["yT"] = dram_tmp(nc, "yT", [8, 128, S])
    c.X = X
    c.dbg = dbg

    run = phases if phases is not None else ["w", "l0in", "gqa", "mgate", "mlstm", "wout0", "cross0", "ffn0",
                                              "l1in", "na", "diff", "wout1", "cross1", "ffn1"]
    run = [ph for ph in run if ph in PHASES]
    for ph in run:
        ar.reset()
        PHASES[ph](c)
        sch.barrier()
    sch.emit()
    return nc


PHASES = {}


def phase(name):
    def deco(f):
        PHASES[name] = f
        return f
    return deco


@phase("w")
def p_weights(c):
    sch, ar, I, W = c.sch, c.ar, c.I, c.W
    gains = ar.alloc("gains", [128, 8, 8])
    glist = [("norm_mix", 0), ("norm_mix", 1), ("norm_cross", 0), ("norm_cross", 1),
             ("norm_mem", 0), ("norm_mem", 1), ("norm_ffn", 0), ("norm_ffn", 1)]
    for gi, (nm, l) in enumerate(glist):
        sch.dma(gains[:, gi, :], I[nm][l].rearrange("(c p) -> p c", p=128), gains, writes=[gains], slow=True)
    jobs = [("ev_w_in", I["ev_w_in"], 0), ("od_w_in", I["od_w_in"], 1),
            ("ev_w_out", I["ev_w_out"], None), ("od_w_out", I["od_w_out"], None),
            ("ca_w_q0", I["ca_w_q"][0], 2), ("ca_w_q1", I["ca_w_q"][1], 3),
            ("ca_w_kv0", I["ca_w_kv"][0], 4), ("ca_w_kv1", I["ca_w_kv"][1], 5),
            ("ca_w_o0", I["ca_w_o"][0], None), ("ca_w_o1", I["ca_w_o"][1], None),
            ("ffn_w_up0", I["ffn_w_up"][0], 6), ("ffn_w_up1", I["ffn_w_up"][1], 7),
            ("ffn_w_down0", I["ffn_w_down"][0], None), ("ffn_w_down1", I["ffn_w_down"][1], None)]
    stg = [ar.alloc("wstg", [128, 2816]) for _ in range(3)]
    outb = [ar.alloc("wout", [128, 2816], BF16) for _ in range(3)]
    n = 0
    for nm, src, gi in jobs:
        K, N = src.shape
        for kc in range(K // 128):
            for c0 in range(0, N, 2816):
                cw = min(2816, N - c0)
                s_ = stg[n % 3]
                o_ = outb[n % 3]
                sch.dma(s_[:, 0:cw], src[kc * 128:(kc + 1) * 128, c0:c0 + cw], s_, writes=[s_])
                if gi is None:
                    if n % 2 == 0:
                        sch.op("dve", lambda e, o=o_, s=s_, cw=cw: e.tensor_copy(out=o[:, 0:cw], in_=s[:, 0:cw]),
                               reads=[s_], writes=[o_])
                    else:
                        sch.op("act", lambda e, o=o_, s=s_, cw=cw: e.activation(out=o[:, 0:cw], in_=s[:, 0:cw], func=AF.Copy),
                               reads=[s_], writes=[o_])
                else:
                    g = gains[:, gi, kc:kc + 1]
                    if n % 2 == 0:
                        sch.op("dve", lambda e, o=o_, s=s_, cw=cw, g=g: e.tensor_scalar(
                            out=o[:, 0:cw], in0=s[:, 0:cw], scalar1=g, scalar2=None, op0=ALU.mult),
                            reads=[s_, gains], writes=[o_])
                    else:
                        sch.op("act", lambda e, o=o_, s=s_, cw=cw, g=g: e.activation(
                            out=o[:, 0:cw], in_=s[:, 0:cw], func=AF.Copy, scale=g),
                            reads=[s_, gains], writes=[o_])
                sch.dma(W[nm][kc * 128:(kc + 1) * 128, c0:c0 + cw], o_[:, 0:cw], o_, reads=[o_], queue="pool")
                n += 1


def load_consts(c, want_ident=True):
    sch, ar, I = c.sch, c.ar, c.I
    idf = ar.alloc("idf", [128, 128])
    idb = ar.alloc("idb", [128, 128], BF16)
    sch.dma(idf[:, :], I["c_ident"], idf, writes=[idf])
    sch.op("dve", lambda e: e.tensor_copy(out=idb[:, :], in_=idf[:, :]), reads=[idf], writes=[idb])
    c.idf, c.idb = idf, idb


def load_weight(c, name, K, N, tag="wsb"):
    sch, ar = c.sch, c.ar
    w = ar.alloc(tag, [128, K // 128, N], BF16)
    src = c.W[name].rearrange("(k p) n -> p k n", p=128)
    kc = K // 128
    step = max(1, kc // 4)
    for k0 in range(0, kc, step):
        k1 = min(kc, k0 + step)
        sch.dma(w[:, k0:k1, :], src[:, k0:k1, :], w, writes=[w])
    return w


def rms_rstd(c, ss, out, n, width, extra_ln=0.0, eng_reads=()):
    sch = c.sch
    sch.op("act", lambda e: e.activation(out=out[:, 0:width], in_=ss[:, 0:width], func=AF.Ln,
                                         scale=1.0 / n, bias=c.epsb[:, 0:1]), reads=[ss, c.epsb], writes=[out])
    if extra_ln == 0.0:
        sch.op("act", lambda e: e.activation(out=out[:, 0:width], in_=out[:, 0:width], func=AF.Exp, scale=-0.5),
               reads=[out], writes=[out])
    else:
        bb = c.lnb[extra_ln]
        sch.op("act", lambda e: e.activation(out=out[:, 0:width], in_=out[:, 0:width], func=AF.Exp, scale=-0.5,
                                             bias=bb[:, 0:1]), reads=[out, bb], writes=[out])


def setup_small(c, ln_consts=()):
    sch, ar = c.sch, c.ar
    c.epsb = ar.alloc("epsb", [128, 1])
    sch.op("dve", lambda e: e.memset(c.epsb[:, :], EPS), writes=[c.epsb])
    c.lnb = {}
    for v in ln_consts:
        t = ar.alloc("lnb", [128, 1])
        sch.op("dve", lambda e, t=t, v=v: e.memset(t[:, :], v), writes=[t])
        c.lnb[v] = t


def norm_transpose(c, src, t0, ntok, hT, col0, xt, xn, ss, rs, pst, out_dt=BF16):
    sch = c.sch
    if ntok < 128:
        sch.op("dve", lambda e: e.memset(xt[:, :], 0.0), writes=[xt])
    sch.dma(xt[0:ntok, :], src[t0:t0 + ntok, :], xt, writes=[xt])
    sch.op("act", lambda e: e.activation(out=xn[:, :], in_=xt[:, :], func=AF.Square, accum_out=ss[:, 0:1]),
           reads=[xt], writes=[xn, ss])
    rms_rstd(c, ss, rs, D, 1)
    sch.op("dve", lambda e: e.tensor_scalar(out=xn[:, :], in0=xt[:, :], scalar1=rs[:, 0:1], scalar2=None, op0=ALU.mult),
           reads=[xt, rs], writes=[xn])
    pv = pst.ap.bitcast(BF16)
    for k in range(8):
        sch.op("pe", lambda e, k=k: e.transpose(out=pv[:, k * 128:(k + 1) * 128], in_=xn[:, k * 128:(k + 1) * 128],
                                                identity=c.idb[:, :]), reads=[xn, c.idb], writes=[pst])
    sch.op("act", lambda e: e.activation(out=hT[:, :, col0:col0 + ntok],
                                         in_=pv.rearrange("p (k t) -> p k t", k=8)[:, :, 0:ntok], func=AF.Copy),
           reads=[pst], writes=[hT])


def bcast_row(c, dst, src_row_ap, n):
    c.sch.dma(dst[:, 0:n], src_row_ap.partition_broadcast(128), dst, writes=[dst])


def p_in(c, layer):
    sch, ar, I, X, S = c.sch, c.ar, c.I, c.X, c.S
    even = (layer == 0)
    load_consts(c)
    setup_small(c, ln_consts=(math.log(0.125),))
    NW = 2832 if even else 3072
    wsb = load_weight(c, "ev_w_in" if even else "od_w_in", D, NW)
    NBK = S // 128
    g = ar.alloc("g", [128, 4, 64])
    gsrc = ([I["ev_attn_qk_gain"][0], I["ev_attn_qk_gain"][1]] if even else
            [I["od_na_qk_gain"][0], I["od_na_qk_gain"][1], I["od_diff_qk_gain"][0], I["od_diff_qk_gain"][1]])
    for i, gs in enumerate(gsrc):
        sch.dma(g[:, i, :], gs.partition_broadcast(128), g, writes=[g])
    if even:
        tabs = []
        gsw = ar.alloc("gsw", [128, 2, 64])
        for i in range(2):
            sch.op("dve", lambda e, i=i: e.tensor_copy(
                out=gsw[:, i, :].rearrange("p (a t d) -> p a t d", a=2, t=2),
                in_=g[:, i, :].rearrange("p (a t d) -> p a t d", a=2, t=2)[:, :, ::-1, :]), reads=[g], writes=[gsw])
        for i in range(2):
            ct = ar.alloc("ct", [128, NBK, 64])
            st = ar.alloc("st", [128, NBK, 64])
            sch.dma(ct[:, :, :], I["c_cos"].rearrange("(n p) d -> p n d", p=128), ct, writes=[ct])
            sch.dma(st[:, :, :], I["c_sin"].rearrange("(n p) d -> p n d", p=128), st, writes=[st])
            sch.op("dve", lambda e, i=i, ct=ct: e.tensor_tensor(
                out=ct[:, :, :], in0=ct[:, :, :], in1=g[:, i:i + 1, :].to_broadcast([128, NBK, 64]), op=ALU.mult),
                reads=[ct, g], writes=[ct])
            sch.op("dve", lambda e, i=i, st=st: e.tensor_tensor(
                out=st[:, :, :], in0=st[:, :, :], in1=gsw[:, i:i + 1, :].to_broadcast([128, NBK, 64]), op=ALU.mult),
                reads=[st, gsw], writes=[st])
            tabs.append((ct, st))
        gbias = ar.alloc("gbias", [16, 1])
        sch.dma(gbias[:, 0:1], I["ev_gate_bias"].rearrange("(p o) -> p o", o=1), gbias, writes=[gbias], slow=True)
    onesb = ar.alloc("onesb", [128, 64], BF16)
    sch.op("dve", lambda e: e.memset(onesb[:, :], 1.0), writes=[onesb])

    hT = [ar.alloc("hT", [128, 8, 512], BF16) for _ in range(2)]
    xt = [ar.alloc("xt", [128, D]) for _ in range(2)]
    xn = [ar.alloc("xn", [128, D], BF16) for _ in range(2)]
    ss = [ar.alloc("ss", [128, 1]) for _ in range(2)]
    rs = [ar.alloc("rs", [128, 1]) for _ in range(2)]
    sq = ar.alloc("sq", [128, 512])
    ssh = ar.alloc("ssh", [128, 8])
    rq = ar.alloc("rq", [128, 8])
    xs = ar.alloc("xs", [128, 512])
    t1 = ar.alloc("t1", [128, 512])
    t2 = ar.alloc("t2", [128, 512])
    qn = [ar.alloc("qn", [128, 512], BF16) for _ in range(2)]
    tok_out = [ar.alloc("tok_out", [128, 8, 130], BF16) for _ in range(3)]
    qTs = [ar.alloc("qTs", [128, 4, 512], BF16) for _ in range(3)]
    fms = [ar.alloc("fms", [128, 512], BF16) for _ in range(3)]
    gst = ar.alloc("gst", [16, 512])
    src = I["x"] if even else X["x1"]
    ps = c.ps
    cnt = {"ps": 0, "to": 0, "qt": 0, "fm": 0, "qn": 0}

    def nxt(k, n):
        v = cnt[k] % n
        cnt[k] += 1
        return v

    def mm_tok(pst, h, j, c0, ncols):
        for k in range(8):
            sch.op("pe", lambda e, k=k: e.matmul(pst[:, 0:ncols], lhsT=h[:, k, j * 128:(j + 1) * 128],
                                                 rhs=wsb[:, k, c0:c0 + ncols], start=(k == 0), stop=(k == 7)),
                   reads=[h, wsb], writes=[pst])

    def qk_epi(pst, nh, gi, rope, lnscale, blk, stage, j, dup):
        w = nh * 64
        sch.op("act", lambda e: e.activation(out=sq[:, 0:w], in_=pst[:, 0:w], func=AF.Square), reads=[pst], writes=[sq])
        sch.op("dve", lambda e: e.tensor_reduce(out=ssh[:, 0:nh], in_=sq[:, 0:w].rearrange("p (h d) -> p h d", h=nh),
                                                axis=AX.X, op=ALU.add), reads=[sq], writes=[ssh])
        rms_rstd(c, ssh, rq, 64, nh, extra_ln=lnscale)
        q_ = qn[nxt("qn", 2)]
        rqb = rq[:, 0:nh].unsqueeze(2).to_broadcast([128, nh, 64])
        if rope:
            ct, st = tabs[gi]
            sch.op("dve", lambda e: e.tensor_copy(
                out=xs[:, 0:w].rearrange("p (a t d) -> p a t d", t=2, d=16),
                in_=pst[:, 0:w].rearrange("p (a t d) -> p a t d", t=2, d=16)[:, :, ::-1, :]), reads=[pst], writes=[xs])
            sch.op("dve", lambda e: e.tensor_tensor(
                out=t1[:, 0:w].rearrange("p (h d) -> p h d", h=nh), in0=pst[:, 0:w].rearrange("p (h d) -> p h d", h=nh),
                in1=ct[:, blk:blk + 1, :].to_broadcast([128, nh, 64]), op=ALU.mult), reads=[pst, ct], writes=[t1])
            sch.op("pool", lambda e: e.tensor_tensor(
                out=t2[:, 0:w].rearrange("p (h d) -> p h d", h=nh), in0=xs[:, 0:w].rearrange("p (h d) -> p h d", h=nh),
                in1=st[:, blk:blk + 1, :].to_broadcast([128, nh, 64]), op=ALU.mult), reads=[xs, st], writes=[t2])
            sch.op("dve", lambda e: e.tensor_tensor(out=t1[:, 0:w], in0=t1[:, 0:w], in1=t2[:, 0:w], op=ALU.add),
                   reads=[t1, t2], writes=[t1])
        else:
            sch.op("dve", lambda e: e.tensor_tensor(
                out=t1[:, 0:w].rearrange("p (h d) -> p h d", h=nh), in0=pst[:, 0:w].rearrange("p (h d) -> p h d", h=nh),
                in1=g[:, gi:gi + 1, :].to_broadcast([128, nh, 64]), op=ALU.mult), reads=[pst, g], writes=[t1])
        if dup:
            for r in range(2):
                sch.op("dve", lambda e, r=r: e.tensor_tensor(
                    out=q_[:, 0:256].rearrange("p (h r d) -> p h r d", h=2, r=2)[:, :, r, :],
                    in0=t1[:, 0:128].rearrange("p (h d) -> p h d", h=2),
                    in1=rq[:, 0:2].unsqueeze(2).to_broadcast([128, 2, 64]), op=ALU.mult), reads=[t1, rq], writes=[q_])
            ngrp = 2
        else:
            sch.op("dve", lambda e: e.tensor_tensor(
                out=q_[:, 0:w].rearrange("p (h d) -> p h d", h=nh), in0=t1[:, 0:w].rearrange("p (h d) -> p h d", h=nh),
                in1=rqb, op=ALU.mult), reads=[t1, rq], writes=[q_])
            ngrp = w // 128
        pt = ps[4 + nxt("ps", 4)]
        pv = pt.ap.bitcast(BF16)
        for gp in range(ngrp):
            sch.op("pe", lambda e, gp=gp: e.transpose(out=pv[:, gp * 128:(gp + 1) * 128], in_=q_[:, gp * 128:(gp + 1) * 128],
                                                      identity=c.idb[:, :]), reads=[q_, c.idb], writes=[pt])
        sch.op("act", lambda e: e.activation(out=stage[:, 0:ngrp, j * 128:(j + 1) * 128],
                                             in_=pv[:, 0:ngrp * 128].rearrange("p (g t) -> p g t", g=ngrp), func=AF.Copy),
               reads=[pt], writes=[stage])

    for ti in range(c.NT):
        t0 = ti * 512
        h = hT[ti % 2]
        for j in range(4):
            norm_transpose(c, src, t0 + j * 128, 128, h, j * 128, xt[j % 2], xn[j % 2], ss[j % 2], rs[j % 2], ps[j % 2])
        if even:
            blocks = [("qk", 0, 8, 0, True, math.log(0.125), "qT", 0, False),
                      ("qk", 512, 2, 1, True, 0.0, "kT", 0, True),
                      ("v", 640, 2, None, None, None, "v", 0, None),
                      ("tokp", 1280, 512, None, None, None, "km", 0, None),
                      ("vm", 1792, 4, None, None, None, "vm", 0, None),
                      ]
            fmb = ([(768 + 128 * i, "qmT", i, 1.0) for i in range(4)] + [(1280 + 128 * i, "kmT", i, 1.0) for i in range(4)]
                   + [(2304 + 128 * i, "omT", i, 1.0) for i in range(4)])
        else:
            blocks = [("qk", 0, 8, 0, False, math.log(0.125), "qT", 0, False),
                      ("qk", 512, 8, 1, False, 0.0, "kT", 0, False),
                      ("v", 1024, 8, None, None, None, "v", 0, None),
                      ("qk", 1536, 8, 2, False, math.log(0.125), "qT2", 0, False),
                      ("qk", 2048, 8, 3, False, 0.0, "kT2", 0, False),
                      ("tokp", 2560, 512, None, None, None, "v2", 0, None)]
            fmb = []
        stages = {}
        for (kind, c0, a, gi, rope, lns, dst, _, dup) in blocks:
            if kind == "qk":
                stages[(dst, c0)] = qTs[nxt("qt", 3)]
            for j in range(4):
                blk = (t0 // 128) + j
                pst = ps[2 + (cnt["to"] % 2)]
                cnt["to"] += 1
                r0 = t0 + j * 128
                if kind == "qk":
                    mm_tok(pst, h, j, c0, a * 64)
                    qk_epi(pst, a, gi, rope, lns, blk, stages[(dst, c0)], j, dup)
                elif kind == "v":
                    mm_tok(pst, h, j, c0, a * 64)
                    cnt["to2"] = cnt.get("to2", 0) + 1
                    to = tok_out[cnt["to2"] % 3]
                    sch.op("act", lambda e, to=to, pst=pst, a=a: e.activation(
                        out=to[:, 0:a, 0:64], in_=pst[:, 0:a * 64].rearrange("p (h d) -> p h d", h=a), func=AF.Copy),
                        reads=[pst], writes=[to])
                    sch.op("pool", lambda e, to=to, a=a: e.tensor_copy(
                        out=to[:, 0:a, 64:128], in_=onesb[:, :].unsqueeze(1).to_broadcast([128, a, 64])),
                        reads=[onesb], writes=[to])
                    sch.dma(X["v"][r0:r0 + 128, 0:a, :], to[:, 0:a, 0:128], to, reads=[to], queue="pool")
                elif kind == "vm":
                    mm_tok(pst, h, j, c0, 512)
                    cnt["to2"] = cnt.get("to2", 0) + 1
                    to = tok_out[cnt["to2"] % 3]
                    sch.op("act", lambda e, to=to, pst=pst: e.activation(
                        out=to[:, 0:4, 0:128], in_=pst[:, 0:512].rearrange("p (h d) -> p h d", h=4), func=AF.Copy),
                        reads=[pst], writes=[to])
                    sch.op("pool", lambda e, to=to: e.memset(to[:, 0:4, 128:129], 1.0), writes=[to])
                    sch.op("pool", lambda e, to=to: e.memset(to[:, 0:4, 129:130], 0.0), writes=[to])
                    sch.dma(X["vm"][r0:r0 + 128, :, :], to[:, 0:4, :], to, reads=[to], queue="pool")
                elif kind == "tokp":
                    mm_tok(pst, h, j, c0, 512)
                    cnt["to2"] = cnt.get("to2", 0) + 1
                    to = tok_out[cnt["to2"] % 3]
                    tf = to.ap.rearrange("p a b -> p (a b)")
                    scl = (128.0 ** -0.5) if dst == "km" else 1.0
                    sch.op("act", lambda e, tf=tf, pst=pst, scl=scl: e.activation(
                        out=tf[:, 0:512], in_=pst[:, 0:512], func=AF.Copy, scale=scl), reads=[pst], writes=[to])
                    dd = X[dst] if dst != "v2" else X["v2"]
                    sch.dma(dd[r0:r0 + 128, :] if dst != "v2" else dd[r0:r0 + 128, :, :].rearrange("t h d -> t (h d)"),
                            tf[:, 0:512], to, reads=[to], queue="pool")
            if kind == "qk":
                stg = stages[(dst, c0)]
                ng = 2 if dup else (a * 64) // 128
                sch.dma(X[dst][0:ng, :, t0:t0 + 512].rearrange("g p t -> p g t"), stg[:, 0:ng, :], stg, reads=[stg],
                        queue="pool")
        for (c0, dst, gi, _) in fmb:
            pst = ps[2 + (cnt["to"] % 2)]
            cnt["to"] += 1
            for k in range(8):
                sch.op("pe", lambda e, k=k, pst=pst, c0=c0, h=h: e.matmul(pst[:, 0:512], lhsT=wsb[:, k, c0:c0 + 128],
                                                                          rhs=h[:, k, :], start=(k == 0), stop=(k == 7)),
                       reads=[h, wsb], writes=[pst])
            f_ = fms[nxt("fm", 3)]
            scl = (128.0 ** -0.5) if dst == "kmT" else 1.0
            sch.op("act", lambda e, f_=f_, pst=pst, scl=scl: e.activation(out=f_[:, :], in_=pst[:, 0:512], func=AF.Copy,
                                                                          scale=scl), reads=[pst], writes=[f_])
            sch.dma(X[dst][gi, :, t0:t0 + 512], f_[:, :], f_, reads=[f_], queue="pool")
        if even:
            pst = ps[2 + (cnt["to"] % 2)]
            cnt["to"] += 1
            for k in range(8):
                sch.op("pe", lambda e, k=k, pst=pst, h=h: e.matmul(pst[0:16, 0:512], lhsT=wsb[:, k, 2816:2832],
                                                                   rhs=h[:, k, :], start=(k == 0), stop=(k == 7)),
                       reads=[h, wsb], writes=[pst])
            sch.op("act", lambda e, pst=pst: e.activation(out=gst[:, :], in_=pst[0:16, 0:512], func=AF.Identity,
                                                          bias=gbias[:, 0:1]), reads=[pst, gbias], writes=[gst])
            sch.dma(X["gates"][:, t0:t0 + 512], gst[:, :], gst, reads=[gst], queue="pool")


@phase("l0in")
def p_l0in(c):
    p_in(c, 0)


@phase("l1in")
def p_l1in(c):
    p_in(c, 1)


def p_attn(c, mode):
    sch, ar, I, X, S = c.sch, c.ar, c.I, c.X, c.S
    NB, NT = c.NB, c.NT
    ps = c.ps
    load_consts(c)
    setup_small(c)
    qsrc, ksrc = (X["qT2"], X["kT2"]) if mode == "diff" else (X["qT"], X["kT"])
    kTs = ar.alloc("kTs", [128, S], BF16)
    nv = 1 if mode != "na" else 2
    vsb = [ar.alloc("vsb", [128, NB, 128], BF16) for _ in range(nv)]
    qs = [ar.alloc("qs", [128, 512], BF16) for _ in range(2)]
    pT = [ar.alloc("pT", [128, 1024], BF16) for _ in range(3)]
    rc = ar.alloc("rc", [128, 512])
    rcs = [ar.alloc("rcs", [128, 512]) for _ in range(2)]
    ysb = [ar.alloc("ysb", [128, 512], BF16) for _ in range(2)]
    if mode == "diff":
        onesk = ar.alloc("onesk", [128, 128], BF16)
        sch.op("dve", lambda e: e.memset(onesk[:, :], 1.0), writes=[onesk])
        t5far = ar.alloc("t5far", [128, 8])
        sch.dma(t5far[:, :], I["c_t5far"], t5far, writes=[t5far])
        bstg = ar.alloc("bstg", [128, 512])
        bias_sb = ar.alloc("bias_sb", [128, 6, 512], BF16)
        lam_init = 0.8 - 0.6 * math.exp(-0.3 * 1)
        lv = ar.alloc("lv", [128, 4, 64])
        for i in range(4):
            sch.dma(lv[:, i, :], I["od_diff_lambda"][i].partition_broadcast(128), lv, writes=[lv])
        lt = ar.alloc("lt", [128, 2, 64])
        lsum = ar.alloc("lsum", [128, 2])
        nlam = ar.alloc("nlam", [128, 1])
        sch.op("dve", lambda e: e.tensor_tensor(out=lt[:, :, :], in0=lv[:, 0:4:2, :], in1=lv[:, 1:4:2, :], op=ALU.mult),
               reads=[lv], writes=[lt])
        sch.op("dve", lambda e: e.tensor_reduce(out=lsum[:, :], in_=lt[:, :, :], axis=AX.X, op=ALU.add),
               reads=[lt], writes=[lsum])
        sch.op("act", lambda e: e.activation(out=lsum[:, :], in_=lsum[:, :], func=AF.Exp), reads=[lsum], writes=[lsum])
        sch.op("dve", lambda e: e.tensor_scalar(out=nlam[:, :], in0=lsum[:, 1:2], scalar1=lsum[:, 0:1], scalar2=-lam_init,
                                                op0=ALU.subtract, op1=ALU.add), reads=[lsum], writes=[nlam])
        dgain = ar.alloc("dgain", [128, 4])
        sch.dma(dgain[:, :], I["od_diff_gain"].rearrange("(h p) -> p h", p=128), dgain, writes=[dgain], slow=True)
        sch.op("dve", lambda e: e.tensor_scalar(out=dgain[:, :], in0=dgain[:, :], scalar1=(1.0 - lam_init), scalar2=None,
                                                op0=ALU.mult), reads=[dgain], writes=[dgain])
        y1 = ar.alloc("y1", [128, 512])
        y2 = ar.alloc("y2", [128, 512])
        ysq = ar.alloc("ysq", [128, 512], BF16)
    if mode == "na":
        bstg = ar.alloc("bstg", [128, 512])
        bias_na = [ar.alloc("bias_na", [128, 24, 512], BF16) for _ in range(2)]
    step = 0
    for grp in range(4):
        kg = (grp // 2) if mode == "gqa" else grp
        sch.dma(kTs[:, :], ksrc[kg], kTs, writes=[kTs])
        if mode == "gqa":
            sch.dma(vsb[0][:, :, :], X["v"][:, kg, :].rearrange("(n p) d -> p n d", p=128), vsb[0], writes=[vsb[0]])
        elif mode == "na":
            for r in range(2):
                sch.dma(vsb[r][:, :, :], X["v"][:, 2 * grp + r, :].rearrange("(n p) d -> p n d", p=128), vsb[r],
                        writes=[vsb[r]])
                for var in range(3):
                    for o in range(8):
                        sch.dma(bstg[:, :], I["c_nab"][2 * grp + r, var, o], bstg, writes=[bstg])
                        sch.op("dve", lambda e, r=r, var=var, o=o: e.tensor_copy(out=bias_na[r][:, var * 8 + o, :],
                                                                               in_=bstg[:, :]),
                               reads=[bstg], writes=[bias_na[r]])
        else:
            sch.dma(vsb[0][:, :, :], X["v2"][:, grp, :].rearrange("(n p) d -> p n d", p=128), vsb[0], writes=[vsb[0]])
            for o in range(6):
                sch.dma(bstg[:, :], I["c_t5"][grp, o], bstg, writes=[bstg])
                sch.op("dve", lambda e, o=o: e.tensor_copy(out=bias_sb[:, o, :], in_=bstg[:, :]),
                       reads=[bstg], writes=[bias_sb])
        steps = []
        for qt in range(NT):
            if mode == "na":
                kbs = [kb for kb in range(4 * qt - 2, 4 * qt + 6) if 0 <= kb < NB]
            else:
                kbs = list(range(NB))
            for ki, kb in enumerate(kbs):
                steps.append((qt, kb, ki == 0, ki == len(kbs) - 1))

        def load_q(qt):
            q_ = qs[qt % 2]
            sch.dma(q_[:, :], qsrc[grp, :, qt * 512:qt * 512 + 512], q_, writes=[q_])

        def banks(i):
            par = (step0 + i) % 2
            return ps[2 * par], ps[2 * par + 1], c.psall[:, 1024 * par:1024 * par + 1024]

        def emit_qk(i):
            qt, kb, first, last = steps[i]
            if first and qt + 1 < NT:
                load_q(qt + 1)
            q_ = qs[qt % 2]
            pa, pb, _ = banks(i)
            ks = slice(kb * 128, (kb + 1) * 128)
            need_bias = (mode == "na") or (mode == "diff" and -1 <= kb - 4 * qt <= 4)
            sch.op("pe", lambda e, pa=pa, q_=q_, ks=ks, nb=need_bias: e.matmul(
                pa[:, :], lhsT=kTs[0:64, ks], rhs=q_[0:64, :], start=True, stop=not nb),
                reads=[kTs, q_], writes=[pa])
            sch.op("pe", lambda e, pb=pb, q_=q_, ks=ks, nb=need_bias: e.matmul(
                pb[:, :], lhsT=kTs[64:128, ks], rhs=q_[64:128, :], start=True, stop=not nb),
                reads=[kTs, q_], writes=[pb])
            if need_bias:
                if mode == "na":
                    var = 0 if qt == 0 else (2 if qt == NT - 1 else 1)
                    o = kb - (4 * qt - 2)
                    ba, bb, rd = bias_na[0][:, var * 8 + o, :], bias_na[1][:, var * 8 + o, :], [bias_na[0], bias_na[1]]
                else:
                    o = kb - 4 * qt + 1
                    ba = bb = bias_sb[:, o, :]
                    rd = [bias_sb]
                sch.op("pe", lambda e, pa=pa, ba=ba: e.matmul(pa[:, :], lhsT=c.idb[:, :], rhs=ba, start=False, stop=True),
                       reads=[c.idb] + rd, writes=[pa])
                sch.op("pe", lambda e, pb=pb, bb=bb: e.matmul(pb[:, :], lhsT=c.idb[:, :], rhs=bb, start=False, stop=True),
                       reads=[c.idb] + rd, writes=[pb])

        def accs(qt):
            if mode == "diff":
                return ps[4], ps[5], ps[6], ps[7]
            return ps[4 + 2 * (qt % 2)], ps[5 + 2 * (qt % 2)], None, None

        def emit_exp_pv(i):
            qt, kb, first, last = steps[i]
            pa, pb, pab = banks(i)
            p_ = pT[(step0 + i) % 3]
            oA, oB, sA, sB = accs(qt)
            need_bias = (mode == "na") or (mode == "diff" and -1 <= kb - 4 * qt <= 4)
            if mode == "diff" and not need_bias:
                fb = t5far[:, grp:grp + 1] if kb < 4 * qt else t5far[:, 4 + grp:5 + grp]
                sch.op("act", lambda e, p_=p_, pab=pab, fb=fb: e.activation(out=p_[:, :], in_=pab, func=AF.Exp, bias=fb),
                       reads=[pa, pb, t5far], writes=[p_])
            else:
                sch.op("act", lambda e, p_=p_, pab=pab: e.activation(out=p_[:, :], in_=pab, func=AF.Exp),
                       reads=[pa, pb], writes=[p_])
            va = vsb[0][:, kb, :]
            vb = vsb[nv - 1][:, kb, :]
            sch.op("pe", lambda e, oA=oA, va=va, p_=p_, first=first, last=last: e.matmul(
                oA[:, :], lhsT=va, rhs=p_[:, 0:512], start=first, stop=last), reads=[vsb[0], p_], writes=[oA])
            sch.op("pe", lambda e, oB=oB, vb=vb, p_=p_, first=first, last=last: e.matmul(
                oB[:, :], lhsT=vb, rhs=p_[:, 512:1024], start=first, stop=last), reads=[vsb[nv - 1], p_], writes=[oB])
            if mode == "diff":
                sch.op("pe", lambda e, sA=sA, p_=p_, first=first, last=last: e.matmul(
                    sA[:, :], lhsT=onesk[:, :], rhs=p_[:, 0:512], start=first, stop=last), reads=[onesk, p_], writes=[sA])
                sch.op("pe", lambda e, sB=sB, p_=p_, first=first, last=last: e.matmul(
                    sB[:, :], lhsT=onesk[:, :], rhs=p_[:, 512:1024], start=first, stop=last), reads=[onesk, p_], writes=[sB])

        def emit_final(qt):
            t0 = qt * 512
            oA, oB, sA, sB = accs(qt)
            y_ = ysb[qt % 2]
            if mode != "diff":
                for r, oo in enumerate((oA, oB)):
                    rc_ = rcs[r]
                    sch.op("dve", lambda e, oo=oo, rc_=rc_: e.reciprocal(out=rc_[0:64, :], in_=oo[64:128, :]), reads=[oo], writes=[rc_])
                    sch.op("dve", lambda e, oo=oo, r=r, y_=y_, rc_=rc_: e.tensor_tensor(
                        out=y_[64 * r:64 * r + 64, :], in0=oo[0:64, :], in1=rc_[0:64, :], op=ALU.mult),
                        reads=[oo, rc_], writes=[y_])
                sch.dma(X["yT"][grp, :, t0:t0 + 512], y_[:, :], y_, reads=[y_], queue="pool")
            else:
                sch.op("dve", lambda e: e.reciprocal(out=rc[:, :], in_=sA[:, :]), reads=[sA], writes=[rc])
                sch.op("dve", lambda e: e.tensor_tensor(out=y1[:, :], in0=oA[:, :], in1=rc[:, :], op=ALU.mult),
                       reads=[oA, rc], writes=[y1])
                sch.op("dve", lambda e: e.reciprocal(out=rc[:, :], in_=sB[:, :]), reads=[sB], writes=[rc])
                sch.op("dve", lambda e: e.tensor_tensor(out=y2[:, :], in0=oB[:, :], in1=rc[:, :], op=ALU.mult),
                       reads=[oB, rc], writes=[y2])
                sch.op("dve", lambda e: e.scalar_tensor_tensor(out=y1[:, :], in0=y2[:, :], scalar=nlam[:, 0:1], in1=y1[:, :],
                                                               op0=ALU.mult, op1=ALU.add), reads=[y1, y2, nlam], writes=[y1])
                sch.op("act", lambda e: e.activation(out=ysq[:, :], in_=y1[:, :], func=AF.Square), reads=[y1], writes=[ysq])
                sch.op("pe", lambda e: e.matmul(sA[:, :], lhsT=onesk[:, :], rhs=ysq[:, :], start=True, stop=True),
                       reads=[onesk, ysq], writes=[sA])
                sch.op("act", lambda e: e.activation(out=y2[:, :], in_=sA[:, :], func=AF.Ln, scale=1.0 / 128,
                                                     bias=c.epsb[:, 0:1]), reads=[sA, c.epsb], writes=[y2])
                sch.op("act", lambda e: e.activation(out=y2[:, :], in_=y2[:, :], func=AF.Exp, scale=-0.5),
                       reads=[y2], writes=[y2])
                dg = dgain[:, grp:grp + 1]
                sch.op("dve", lambda e, y_=y_, dg=dg: e.scalar_tensor_tensor(
                    out=y_[:, :], in0=y1[:, :], scalar=dg, in1=y2[:, :], op0=ALU.mult, op1=ALU.mult),
                    reads=[y1, y2, dgain], writes=[y_])
                sch.dma(X["yT"][4 + grp, :, t0:t0 + 512], y_[:, :], y_, reads=[y_], queue="pool")

        step0 = step
        load_q(0)
        emit_qk(0)
        for i in range(len(steps)):
            if i + 1 < len(steps):
                emit_qk(i + 1)
            emit_exp_pv(i)
            if steps[i][3]:
                emit_final(steps[i][0])
        step += len(steps)


@phase("gqa")
def p_gqa(c):
    p_attn(c, "gqa")


@phase("na")
def p_na(c):
    p_attn(c, "na")


@phase("diff")
def p_diff(c):
    p_attn(c, "diff")


@phase("mgate")
def p_mgate(c):
    sch, ar, I, X, S = c.sch, c.ar, c.I, c.X, c.S
    NB = c.NB
    load_consts(c)
    setup_small(c)
    oneb = ar.alloc("oneb", [128, 1])
    sch.op("dve", lambda e: e.memset(oneb[:, :], 1.0), writes=[oneb])
    A = ar.alloc("A", [36, S])
    B = ar.alloc("B", [36, S])
    Cc = ar.alloc("C", [36, S])
    Dd = ar.alloc("D", [36, S])
    for t_ in (A, B, Cc, Dd):
        sch.op("pool", lambda e, t_=t_: e.memset(t_[:, :], 0.0), writes=[t_])
    G = X["gates"]
    sch.dma(A[0:4, :], G[0:4, :], A, writes=[A])
    sch.dma(A[32:36, :], G[8:12, :], A, writes=[A])
    sch.dma(B[0:4, :], G[4:8, :], B, writes=[B])
    sch.dma(B[32:36, :], G[12:16, :], B, writes=[B])
    sch.op("dve", lambda e: e.tensor_copy(out=Cc[0:4, :], in_=A[0:4, :]), reads=[A], writes=[Cc])
    sch.op("dve", lambda e: e.tensor_copy(out=Cc[32:36, :], in_=A[32:36, ::-1]), reads=[A], writes=[Cc])
    sch.op("act", lambda e: e.activation(out=B[:, :], in_=B[:, :], func=AF.Exp, scale=-1.0), reads=[B], writes=[B])
    sch.op("act", lambda e: e.activation(out=B[:, :], in_=B[:, :], func=AF.Ln, bias=oneb[0:36, 0:1]), reads=[B, oneb], writes=[B])
    sch.op("dve", lambda e: e.tensor_copy(out=Dd[0:4, :], in_=B[0:4, :]), reads=[B], writes=[Dd])
    sch.op("dve", lambda e: e.tensor_copy(out=Dd[32:36, :], in_=B[32:36, ::-1]), reads=[B], writes=[Dd])
    sch.op("dve", lambda e: e.tensor_tensor_scan(out=A[:, :], data0=Dd[:, :], data1=Dd[:, :], initial=0.0,
                                                 op0=ALU.add, op1=ALU.max), reads=[Dd], writes=[A])
    sch.op("dve", lambda e: e.tensor_tensor(out=B[:, :], in0=Cc[:, :], in1=A[:, :], op=ALU.add), reads=[Cc, A], writes=[B])
    sch.op("dve", lambda e: e.tensor_tensor_scan(out=Cc[:, :], data0=B[:, :], data1=B[:, :], initial=0.0,
                                                 op0=ALU.max, op1=ALU.max), reads=[B], writes=[Cc])
    sch.op("dve", lambda e: e.tensor_tensor(out=Dd[:, :], in0=Cc[:, :], in1=A[:, :], op=ALU.subtract), reads=[Cc, A], writes=[Dd])
    sch.op("act", lambda e: e.activation(out=Dd[:, :], in_=Dd[:, :], func=AF.Exp, scale=-1.0), reads=[Dd], writes=[Dd])
    sch.op("dve", lambda e: e.tensor_scalar(out=A[:, :], in0=Cc[:, :], scalar1=-1.0, scalar2=None, op0=ALU.mult),
           reads=[Cc], writes=[A])
    sch.op("dve", lambda e: e.tensor_copy(out=Cc[0:4, :], in_=A[0:4, :]), reads=[A], writes=[Cc])
    sch.op("dve", lambda e: e.tensor_copy(out=Cc[32:36, :], in_=A[32:36, ::-1]), reads=[A], writes=[Cc])
    sch.op("dve", lambda e: e.tensor_copy(out=A[0:4, :], in_=Dd[0:4, :]), reads=[Dd], writes=[A])
    sch.op("dve", lambda e: e.tensor_copy(out=A[32:36, :], in_=Dd[32:36, ::-1]), reads=[Dd], writes=[A])
    sch.op("dve", lambda e: e.tensor_copy(out=Dd[0:4, :], in_=B[0:4, :]), reads=[B], writes=[Dd])
    sch.op("dve", lambda e: e.tensor_copy(out=Dd[32:36, :], in_=B[32:36, ::-1]), reads=[B], writes=[Dd])
    for d in range(2):
        sch.dma(X["grow"][4 * d:4 * d + 4, 0, :], Cc[32 * d:32 * d + 4, :], Cc, reads=[Cc], queue="pool")
        sch.dma(X["grow"][4 * d:4 * d + 4, 1, :], A[32 * d:32 * d + 4, :], A, reads=[A], queue="pool")
    ust = [ar.alloc("ust", [128, 4, 36]) for _ in range(2)]
    for kb4 in range(NB // 4):
        pst = c.ps[kb4 % 2]
        u_ = ust[kb4 % 2]
        for j in range(4):
            kb = kb4 * 4 + j
            sch.op("pe", lambda e, pst=pst, j=j, kb=kb: e.transpose(out=pst[:, j * 36:(j + 1) * 36],
                                                                  in_=Dd[0:36, kb * 128:(kb + 1) * 128],
                                                                  identity=c.idf[0:36, 0:36]),
                   reads=[Dd, c.idf], writes=[pst])
        sch.op("act", lambda e, pst=pst, u_=u_: e.activation(out=u_[:, :, :], in_=pst[:, 0:144].rearrange("p (j d) -> p j d", j=4),
                                                          func=AF.Copy), reads=[pst], writes=[u_])
        sch.dma(X["gv"][kb4 * 512:(kb4 + 1) * 512, 0:36].rearrange("(j p) d -> p j d", p=128), u_[:, :, :], u_,
                reads=[u_], queue="pool")


@phase("mlstm")
def p_mlstm(c):
    sch, ar, I, X, S = c.sch, c.ar, c.I, c.X, c.S
    NB, NT = c.NB, c.NT
    ps = c.ps
    load_consts(c)
    setup_small(c)
    oneb = ar.alloc("oneb", [128, 1])
    sch.op("dve", lambda e: e.memset(oneb[:, :], 1.0), writes=[oneb])
    onesk = ar.alloc("onesk", [128, 128], BF16)
    sch.op("dve", lambda e: e.memset(onesk[:, :], 1.0), writes=[onesk])
    onesf = ar.alloc("onesf", [128, 128])
    sch.op("dve", lambda e: e.memset(onesf[:, :], 1.0), writes=[onesf])
    mk = ar.alloc("mk", [128, 8, 512])
    sch.dma(mk[:, :, :], I["c_mmask"].rearrange("o p j -> p o j"), mk, writes=[mk])
    uT = ar.alloc("uT", [128, NB, 36])
    sch.dma(uT[:, :, :], X["gv"][:, 0:36].rearrange("(n p) d -> p n d", p=128), uT, writes=[uT])
    mg = ar.alloc("mg", [128, 4])
    sch.dma(mg[:, :], I["ev_mlstm_gain"].rearrange("(h p) -> p h", p=128), mg, writes=[mg], slow=True)
    kTs = ar.alloc("kTs", [128, S], BF16)
    vsb = ar.alloc("vsb", [128, NB, 128], BF16)
    qs = [ar.alloc("qs", [128, 512], BF16) for _ in range(3)]
    oms = [ar.alloc("oms", [128, 512], BF16) for _ in range(3)]
    rows = [[ar.alloc("rows", [128, 2, 512]) for _ in range(2)] for _ in range(3)]
    wt = [ar.alloc("wt", [128, 512], BF16) for _ in range(4)]
    t1s = [ar.alloc("t1s", [128, 512]) for _ in range(2)]
    tmpm = [ar.alloc("tmpm", [128, 512]) for _ in range(3)]
    sc = [ar.alloc("sc", [128, 512], BF16) for _ in range(4)]
    stb = [ps[0], ps[1], ps[7]]
    hd = [ar.alloc("hd", [128, 512]) for _ in range(2)]
    t1 = ar.alloc("t1", [128, 512])
    t2 = ar.alloc("t2", [128, 512])
    ysb = [ar.alloc("ysb", [128, 512], BF16) for _ in range(2)]
    step = 0
    for h in range(4):
        sch.dma(kTs[:, :], X["kmT"][h], kTs, writes=[kTs])
        sch.dma(vsb[:, :, :], X["vm"][:, h, 0:128].rearrange("(n p) d -> p n d", p=128), vsb, writes=[vsb])
        steps = []
        for qt in range(NT):
            for d in range(2):
                kbs = list(range(0, 4 * qt + 4)) if d == 0 else list(range(4 * qt, NB))
                for ki, kb in enumerate(kbs):
                    steps.append((qt, d, kb, ki == 0, ki == len(kbs) - 1))

        def load_tile(qt):
            t0 = qt * 512
            sch.dma(qs[qt % 3][:, :], X["qmT"][h, :, t0:t0 + 512], qs[qt % 3], writes=[qs[qt % 3]])
            sch.dma(oms[qt % 3][:, :], X["omT"][h, :, t0:t0 + 512], oms[qt % 3], writes=[oms[qt % 3]])
            for d in range(2):
                r_ = rows[qt % 3][d]
                sch.dma(r_[:, :, :], X["grow"][4 * d + h, :, t0:t0 + 512].partition_broadcast(128), r_, writes=[r_])

        def emit_qk(i):
            qt, d, kb, first, last = steps[i]
            if first and d == 0 and qt + 1 < NT:
                load_tile(qt + 1)
            pa = stb[(step0 + i) % 3]
            q_ = qs[qt % 3]
            ks = slice(kb * 128, (kb + 1) * 128)
            sch.op("pe", lambda e, pa=pa, q_=q_, ks=ks: e.matmul(pa[:, :], lhsT=kTs[:, ks], rhs=q_[:, :], start=True, stop=True),
                   reads=[kTs, q_], writes=[pa])

        def emit_rest(i):
            qt, d, kb, first, last = steps[i]
            pa = stb[(step0 + i) % 3]
            w_ = wt[(step0 + i) % 4]
            s_ = sc[(step0 + i) % 4]
            tm = tmpm[(step0 + i) % 3]
            r_ = rows[qt % 3][d]
            oD, dD = ps[2 + 2 * d], ps[3 + 2 * d]
            ub = uT[:, kb, 32 * d + h:32 * d + h + 1]
            o = kb - 4 * qt
            if 0 <= o <= 3:
                mko = mk[:, 4 * d + o, :]
                sch.op("pool", lambda e, tm=tm, r_=r_, mko=mko: e.tensor_tensor(out=tm[:, :], in0=r_[:, 0, :], in1=mko, op=ALU.add),
                       reads=[r_, mk], writes=[tm])
                sch.op("act", lambda e, w_=w_, tm=tm, ub=ub: e.activation(out=w_[:, :], in_=tm[:, :], func=AF.Exp, bias=ub),
                       reads=[tm, uT], writes=[w_])
            else:
                sch.op("act", lambda e, w_=w_, r_=r_, ub=ub: e.activation(out=w_[:, :], in_=r_[:, 0, :], func=AF.Exp, bias=ub),
                       reads=[r_, uT], writes=[w_])
            sch.op("dve", lambda e, s_=s_, pa=pa, w_=w_: e.tensor_tensor(out=s_[:, :], in0=pa[:, :], in1=w_[:, :], op=ALU.mult),
                   reads=[pa, w_], writes=[s_])
            sch.op("pe", lambda e, oD=oD, kb=kb, s_=s_, first=first, last=last: e.matmul(
                oD[:, :], lhsT=vsb[:, kb, :], rhs=s_[:, :], start=first, stop=last), reads=[vsb, s_], writes=[oD])
            sch.op("pe", lambda e, dD=dD, s_=s_, first=first, last=last: e.matmul(
                dD[:, :], lhsT=onesk[:, :], rhs=s_[:, :], start=first, stop=last), reads=[onesk, s_], writes=[dD])

        def emit_final_dir(qt, d):
            r_ = rows[qt % 3][d]
            oD, dD = ps[2 + 2 * d], ps[3 + 2 * d]
            hh = hd[d]
            t1_ = t1s[d]
            sch.op("act", lambda e, dD=dD, t1_=t1_: e.activation(out=t1_[:, :], in_=dD[:, :], func=AF.Abs), reads=[dD], writes=[t1_])
            sch.op("dve", lambda e, r_=r_, t1_=t1_: e.tensor_tensor(out=t1_[:, :], in0=t1_[:, :], in1=r_[:, 1, :], op=ALU.max),
                   reads=[t1_, r_], writes=[t1_])
            sch.op("dve", lambda e, t1_=t1_: e.reciprocal(out=t1_[:, :], in_=t1_[:, :]), reads=[t1_], writes=[t1_])
            sch.op("dve", lambda e, oD=oD, hh=hh, t1_=t1_: e.tensor_tensor(out=hh[:, :], in0=oD[:, :], in1=t1_[:, :], op=ALU.mult),
                   reads=[oD, t1_], writes=[hh])

        def emit_combine(qt):
            t0 = qt * 512
            om_ = oms[qt % 3]
            y_ = ysb[qt % 2]
            sch.op("pool", lambda e: e.tensor_tensor(out=hd[0][:, :], in0=hd[0][:, :], in1=hd[1][:, :], op=ALU.add),
                   reads=[hd[0], hd[1]], writes=[hd[0]])
            sch.op("act", lambda e: e.activation(out=t2[:, :], in_=hd[0][:, :], func=AF.Square), reads=[hd[0]], writes=[t2])
            sch.op("pe", lambda e: e.matmul(ps[6][:, :], lhsT=onesf[:, :], rhs=t2[:, :], start=True, stop=True),
                   reads=[onesf, t2], writes=[ps[6]])
            sch.op("act", lambda e: e.activation(out=t2[:, :], in_=ps[6][:, :], func=AF.Ln, scale=1.0 / 128, bias=c.epsb[:, 0:1]),
                   reads=[ps[6], c.epsb], writes=[t2])
            sch.op("act", lambda e: e.activation(out=t2[:, :], in_=t2[:, :], func=AF.Exp, scale=-0.5), reads=[t2], writes=[t2])
            mgh = mg[:, h:h + 1]
            sch.op("dve", lambda e, mgh=mgh: e.scalar_tensor_tensor(out=hd[0][:, :], in0=hd[0][:, :], scalar=mgh, in1=t2[:, :],
                                                                     op0=ALU.mult, op1=ALU.mult), reads=[hd[0], t2, mg], writes=[hd[0]])
            sch.op("act", lambda e, om_=om_: e.activation(out=t2[:, :], in_=om_[:, :], func=AF.Exp, scale=-1.0), reads=[om_], writes=[t2])
            sch.op("pool", lambda e: e.tensor_scalar(out=t2[:, :], in0=t2[:, :], scalar1=1.0, scalar2=None, op0=ALU.add),
                   reads=[t2], writes=[t2])
            sch.op("dve", lambda e: e.reciprocal(out=t2[:, :], in_=t2[:, :]), reads=[t2], writes=[t2])
            sch.op("pool", lambda e, y_=y_: e.tensor_tensor(out=y_[:, :], in0=hd[0][:, :], in1=t2[:, :], op=ALU.mult),
                   reads=[hd[0], t2], writes=[y_])
            sch.dma(X["yT"][4 + h, :, t0:t0 + 512], y_[:, :], y_, reads=[y_], queue="pool")

        step0 = step
        load_tile(0)
        emit_qk(0)
        emit_qk(1)
        for i in range(len(steps)):
            if i + 2 < len(steps):
                emit_qk(i + 2)
            emit_rest(i)
            qt, d, kb, first, last = steps[i]
            if last:
                emit_final_dir(qt, d)
                if d == 1:
                    emit_combine(qt)
        step += len(steps)


def p_wout(c, layer, src, dst):
    sch, ar, I, X, S = c.sch, c.ar, c.I, c.X, c.S
    ps = c.ps
    wsb = load_weight(c, "ev_w_out" if layer == 0 else "od_w_out", D, D)
    ys = [ar.alloc("ys", [128, 8, 512], BF16) for _ in range(2)]
    xt = [ar.alloc("xt", [128, D]) for _ in range(3)]
    n = 0
    for ti in range(c.NT):
        t0 = ti * 512
        y_ = ys[ti % 2]
        sch.dma(y_[:, :, :], X["yT"][:, :, t0:t0 + 512].rearrange("k p t -> p k t"), y_, writes=[y_])
        for j in range(4):
            r0 = t0 + j * 128
            x_ = xt[n % 3]
            n += 1
            sch.dma(x_[:, :], src[r0:r0 + 128, :], x_, writes=[x_])
            for half in range(2):
                pst = ps[(2 * j + half) % 4]
                for k in range(8):
                    sch.op("pe", lambda e, pst=pst, y_=y_, k=k, j=j, half=half: e.matmul(
                        pst[:, :], lhsT=y_[:, k, j * 128:(j + 1) * 128], rhs=wsb[:, k, half * 512:(half + 1) * 512],
                        start=(k == 0), stop=(k == 7)), reads=[y_, wsb], writes=[pst])
                sch.op("dve", lambda e, pst=pst, x_=x_, half=half: e.tensor_tensor(
                    out=x_[:, half * 512:(half + 1) * 512], in0=pst[:, :], in1=x_[:, half * 512:(half + 1) * 512], op=ALU.add),
                    reads=[pst, x_], writes=[x_])
            sch.dma(dst[r0:r0 + 128, :], x_[:, :], x_, reads=[x_], queue="pool")


def p_cross(c, layer, src, dst):
    sch, ar, I, X, S = c.sch, c.ar, c.I, c.X, c.S
    ps = c.ps
    load_consts(c)
    setup_small(c, ln_consts=(math.log(128.0 ** -0.5),))
    wq = load_weight(c, "ca_w_q%d" % layer, D, 512, tag="wq")
    wo = load_weight(c, "ca_w_o%d" % layer, 512, D, tag="wo")
    wkv = load_weight(c, "ca_w_kv%d" % layer, D, 1024, tag="wkv")
    g = ar.alloc("g", [128, 2, 128])
    for i in range(2):
        sch.dma(g[:, i, :], I["ca_qk_gain"][layer, i].partition_broadcast(128), g, writes=[g])
    onesk = ar.alloc("onesk", [128, 128], BF16)
    sch.op("dve", lambda e: e.memset(onesk[:, :], 1.0), writes=[onesk])
    xt = [ar.alloc("xt", [128, D]) for _ in range(8)]
    xn = [ar.alloc("xn", [128, D], BF16) for _ in range(2)]
    ss = [ar.alloc("ss", [128, 1]) for _ in range(2)]
    rs = [ar.alloc("rs", [128, 1]) for _ in range(2)]
    hT = [ar.alloc("hT", [128, 8, 512], BF16) for _ in range(2)]
    sq = ar.alloc("sq", [128, 512])
    ssh = ar.alloc("ssh", [128, 4])
    rq = ar.alloc("rq", [128, 4])
    t1 = ar.alloc("t1", [128, 512])
    qn = [ar.alloc("qn", [128, 512], BF16) for _ in range(2)]
    KT = ar.alloc("KT", [128, 4, MEM], BF16)
    V = ar.alloc("V", [128, 2, 512], BF16)
    QT = [ar.alloc("QT", [128, 4, 512], BF16) for _ in range(2)]
    OT = [ar.alloc("OT", [128, 4, 512], BF16) for _ in range(2)]
    pT = [ar.alloc("pT", [128, 512], BF16) for _ in range(3)]
    rc = ar.alloc("rc", [128, 512])

    def headnorm_T(pst, gi, lnscale, stage, col0, qi):
        sch.op("act", lambda e: e.activation(out=sq[:, :], in_=pst[:, :], func=AF.Square), reads=[pst], writes=[sq])
        sch.op("dve", lambda e: e.tensor_reduce(out=ssh[:, :], in_=sq[:, :].rearrange("p (h d) -> p h d", h=4), axis=AX.X,
                                                op=ALU.add), reads=[sq], writes=[ssh])
        rms_rstd(c, ssh, rq, 128, 4, extra_ln=lnscale)
        q_ = qn[qi % 2]
        sch.op("dve", lambda e: e.tensor_tensor(out=t1[:, :].rearrange("p (h d) -> p h d", h=4),
                                                in0=pst[:, :].rearrange("p (h d) -> p h d", h=4),
                                                in1=g[:, gi:gi + 1, :].to_broadcast([128, 4, 128]), op=ALU.mult),
               reads=[pst, g], writes=[t1])
        sch.op("dve", lambda e: e.tensor_tensor(out=q_[:, :].rearrange("p (h d) -> p h d", h=4),
                                                in0=t1[:, :].rearrange("p (h d) -> p h d", h=4),
                                                in1=rq[:, 0:4].unsqueeze(2).to_broadcast([128, 4, 128]), op=ALU.mult),
               reads=[t1, rq], writes=[q_])
        pt = ps[6 + qi % 2]
        pv = pt.ap.bitcast(BF16)
        for h in range(4):
            sch.op("pe", lambda e, h=h: e.transpose(out=pv[:, h * 128:(h + 1) * 128], in_=q_[:, h * 128:(h + 1) * 128],
                                                    identity=c.idb[:, :]), reads=[q_, c.idb], writes=[pt])
        sch.op("act", lambda e: e.activation(out=stage[:, :, col0:col0 + 128],
                                             in_=pv[:, 0:512].rearrange("p (h t) -> p h t", h=4), func=AF.Copy),
               reads=[pt], writes=[stage])

    mT = hT[1]
    for b in range(2):
        norm_transpose(c, I["mem"], b * 128, 128, mT, b * 128, xt[b], xn[b], ss[b], rs[b], ps[b])
    for b in range(2):
        pk, pvv = ps[2 + b], ps[4 + b]
        for k in range(8):
            sch.op("pe", lambda e, k=k, pk=pk, b=b: e.matmul(pk[:, :], lhsT=mT[:, k, b * 128:(b + 1) * 128], rhs=wkv[:, k, 0:512],
                                                             start=(k == 0), stop=(k == 7)), reads=[mT, wkv], writes=[pk])
        for k in range(8):
            sch.op("pe", lambda e, k=k, pvv=pvv, b=b: e.matmul(pvv[:, :], lhsT=mT[:, k, b * 128:(b + 1) * 128], rhs=wkv[:, k, 512:1024],
                                                               start=(k == 0), stop=(k == 7)), reads=[mT, wkv], writes=[pvv])
        headnorm_T(pk, 1, 0.0, KT, b * 128, b)
        sch.op("act", lambda e, pvv=pvv, b=b: e.activation(out=V[:, b, :], in_=pvv[:, :], func=AF.Copy), reads=[pvv], writes=[V])
    step = 0
    qi = 0
    for ti in range(c.NT):
        t0 = ti * 512
        h_ = hT[0]
        xs_ = [xt[4 * (ti % 2) + j] for j in range(4)]
        for j in range(4):
            norm_transpose(c, src, t0 + j * 128, 128, h_, j * 128, xs_[j], xn[j % 2], ss[j % 2], rs[j % 2], ps[j % 2])
        qT_, oT_ = QT[ti % 2], OT[ti % 2]
        for j in range(4):
            pst = ps[2 + j % 2]
            for k in range(8):
                sch.op("pe", lambda e, k=k, pst=pst, j=j: e.matmul(pst[:, :], lhsT=h_[:, k, j * 128:(j + 1) * 128], rhs=wq[:, k, :],
                                                                   start=(k == 0), stop=(k == 7)), reads=[h_, wq], writes=[pst])
            headnorm_T(pst, 0, math.log(128.0 ** -0.5), qT_, j * 128, qi)
            qi += 1
        for h in range(4):
            oD, sD = ps[4], ps[5]
            for kb in range(2):
                pa = ps[step % 2]
                p_ = pT[step % 3]
                step += 1
                sch.op("pe", lambda e, pa=pa, h=h, kb=kb, qT_=qT_: e.matmul(pa[:, :], lhsT=KT[:, h, kb * 128:(kb + 1) * 128],
                                                                           rhs=qT_[:, h, :], start=True, stop=True),
                       reads=[KT, qT_], writes=[pa])
                sch.op("act", lambda e, pa=pa, p_=p_: e.activation(out=p_[:, :], in_=pa[:, :], func=AF.Exp), reads=[pa], writes=[p_])
                sch.op("pe", lambda e, oD=oD, h=h, kb=kb, p_=p_: e.matmul(oD[:, :], lhsT=V[:, kb, h * 128:(h + 1) * 128], rhs=p_[:, :],
                                                                         start=(kb == 0), stop=(kb == 1)), reads=[V, p_], writes=[oD])
                sch.op("pe", lambda e, sD=sD, p_=p_, kb=kb: e.matmul(sD[:, :], lhsT=onesk[:, :], rhs=p_[:, :],
                                                                    start=(kb == 0), stop=(kb == 1)), reads=[onesk, p_], writes=[sD])
            sch.op("dve", lambda e, sD=sD: e.reciprocal(out=rc[:, :], in_=sD[:, :]), reads=[sD], writes=[rc])
            sch.op("dve", lambda e, oD=oD, oT_=oT_, h=h: e.tensor_tensor(out=oT_[:, h, :], in0=oD[:, :], in1=rc[:, :], op=ALU.mult),
                   reads=[oD, rc], writes=[oT_])
        for j in range(4):
            r0 = t0 + j * 128
            x_ = xs_[j]
            for half in range(2):
                pst = ps[2 + half]
                for h in range(4):
                    sch.op("pe", lambda e, pst=pst, oT_=oT_, h=h, j=j, half=half: e.matmul(
                        pst[:, :], lhsT=oT_[:, h, j * 128:(j + 1) * 128], rhs=wo[:, h, half * 512:(half + 1) * 512],
                        start=(h == 0), stop=(h == 3)), reads=[oT_, wo], writes=[pst])
                sch.op("dve", lambda e, pst=pst, x_=x_, half=half: e.tensor_tensor(
                    out=x_[:, half * 512:(half + 1) * 512], in0=pst[:, :], in1=x_[:, half * 512:(half + 1) * 512], op=ALU.add),
                    reads=[pst, x_], writes=[x_])
            sch.dma(dst[r0:r0 + 128, :], x_[:, :], x_, reads=[x_], queue="pool")


def p_ffn(c, layer, src, dst):
    sch, ar, I, X, S = c.sch, c.ar, c.I, c.X, c.S
    ps = c.ps
    NCH = D_FF // 128
    load_consts(c)
    setup_small(c)
    wup = load_weight(c, "ffn_w_up%d" % layer, D, 2 * D_FF, tag="wup")
    wdn = load_weight(c, "ffn_w_down%d" % layer, D_FF, D, tag="wdn")
    cw = ar.alloc("cw", [128, 3, 2 * NCH])
    cb = ar.alloc("cb", [128, 2 * NCH])
    for j in range(3):
        sch.dma(cw[:, j, :], I["ffn_conv_w"][layer, j].rearrange("(c p) -> p c", p=128), cw, writes=[cw], slow=True)
    sch.dma(cb[:, :], I["ffn_conv_b"][layer].rearrange("(c p) -> p c", p=128), cb, writes=[cb], slow=True)
    xt = [ar.alloc("xt", [128, D]) for _ in range(4)]
    xh = ar.alloc("xh", [128, D])
    xn = [ar.alloc("xn", [128, D], BF16) for _ in range(2)]
    ss = [ar.alloc("ss", [128, 1]) for _ in range(2)]
    rs = [ar.alloc("rs", [128, 1]) for _ in range(2)]
    hT = ar.alloc("hT", [128, 8, 514], BF16)
    G = ar.alloc("G", [128, NCH, 512], BF16)
    cg = [ar.alloc("cg", [128, 512]) for _ in range(2)]
    cv = [ar.alloc("cv", [128, 512]) for _ in range(2)]
    sg = [ar.alloc("sg", [128, 512]) for _ in range(2)]
    n = 0
    for ti in range(c.NT):
        t0 = ti * 512
        for j in range(4):
            norm_transpose(c, src, t0 + j * 128, 128, hT, 1 + j * 128, xt[j], xn[j % 2], ss[j % 2], rs[j % 2], ps[j % 2])
        sch.op("pool", lambda e: e.memset(xh[:, :], 0.0), writes=[xh])
        if t0 > 0:
            sch.dma(xh[0:1, :], src[t0 - 1:t0, :], xh, writes=[xh])
        if t0 + 512 < S:
            sch.dma(xh[1:2, :], src[t0 + 512:t0 + 513, :], xh, writes=[xh])
        sch.op("act", lambda e: e.activation(out=xn[0][:, :], in_=xh[:, :], func=AF.Square, accum_out=ss[0][:, 0:1]),
               reads=[xh], writes=[xn[0], ss[0]])
        rms_rstd(c, ss[0], rs[0], D, 1)
        sch.op("dve", lambda e: e.tensor_scalar(out=xn[0][:, :], in0=xh[:, :], scalar1=rs[0][:, 0:1], scalar2=None, op0=ALU.mult),
               reads=[xh, rs[0]], writes=[xn[0]])
        pvh = ps[0].ap.bitcast(BF16)
        for k in range(8):
            sch.op("pe", lambda e, k=k: e.transpose(out=pvh[:, k * 128:(k + 1) * 128], in_=xn[0][:, k * 128:(k + 1) * 128],
                                                    identity=c.idb[:, :]), reads=[xn[0], c.idb], writes=[ps[0]])
        sch.op("act", lambda e: e.activation(out=hT[:, :, 0:1], in_=pvh.rearrange("p (k t) -> p k t", k=8)[:, :, 0:1], func=AF.Copy),
               reads=[ps[0]], writes=[hT])
        sch.op("act", lambda e: e.activation(out=hT[:, :, 513:514], in_=pvh.rearrange("p (k t) -> p k t", k=8)[:, :, 1:2], func=AF.Copy),
               reads=[ps[0]], writes=[hT])
        for f in range(NCH):
            outs = []
            for which in range(2):
                ch = f + which * NCH
                pp = 2 * ((2 * f + which) % 4)
                pa, pb = ps[pp], ps[pp + 1]
                for half, pst in enumerate((pa, pb)):
                    for k in range(8):
                        sch.op("pe", lambda e, k=k, pst=pst, ch=ch, half=half: e.matmul(
                            pst[:, 0:258], lhsT=wup[:, k, ch * 128:(ch + 1) * 128], rhs=hT[:, k, 256 * half:256 * half + 258],
                            start=(k == 0), stop=(k == 7)), reads=[hT, wup], writes=[pst])
                pv2 = c.psall[:, pp * 512:(pp + 2) * 512].rearrange("p (a b) -> p a b", a=2)
                dst_c = (cg if which == 0 else cv)[n % 2]
                dv = dst_c[:, :].rearrange("p (a b) -> p a b", a=2)
                sch.op("act", lambda e, dv=dv, pv2=pv2, ch=ch: e.activation(
                    out=dv, in_=pv2[:, :, 1:257], func=AF.Identity, scale=cw[:, 1, ch:ch + 1], bias=cb[:, ch:ch + 1]),
                    reads=[pa, pb, cw, cb], writes=[dst_c])
                for sh, wi in ((0, 0), (2, 2)):
                    sch.op("dve", lambda e, dv=dv, pv2=pv2, ch=ch, sh=sh, wi=wi: e.scalar_tensor_tensor(
                        out=dv, in0=pv2[:, :, sh:sh + 256], scalar=cw[:, wi, ch:ch + 1], in1=dv, op0=ALU.mult, op1=ALU.add),
                        reads=[pa, pb, cw, dst_c], writes=[dst_c])
                outs.append(dst_c)
            s_ = sg[n % 2]
            sch.op("act", lambda e, s_=s_, a=outs[0]: e.activation(out=s_[:, :], in_=a[:, :], func=AF.Silu), reads=[outs[0]], writes=[s_])
            sch.op("dve", lambda e, s_=s_, b=outs[1], f=f: e.tensor_tensor(out=G[:, f, :], in0=s_[:, :], in1=b[:, :], op=ALU.mult),
                   reads=[s_, outs[1]], writes=[G])
            n += 1
        for j in range(4):
            r0 = t0 + j * 128
            x_ = xt[j]
            for half in range(2):
                pst = ps[(2 * j + half) % 8]
                for f in range(NCH):
                    sch.op("pe", lambda e, pst=pst, f=f, j=j, half=half: e.matmul(
                        pst[:, :], lhsT=G[:, f, j * 128:(j + 1) * 128], rhs=wdn[:, f, half * 512:(half + 1) * 512],
                        start=(f == 0), stop=(f == NCH - 1)), reads=[G, wdn], writes=[pst])
                sch.op("dve", lambda e, pst=pst, x_=x_, half=half: e.tensor_tensor(
                    out=x_[:, half * 512:(half + 1) * 512], in0=pst[:, :], in1=x_[:, half * 512:(half + 1) * 512], op=ALU.add),
                    reads=[pst, x_], writes=[x_])
            sch.dma(dst[r0:r0 + 128, :], x_[:, :], x_, reads=[x_], queue="pool")


@phase("wout0")
def _p(c):
    p_wout(c, 0, c.I["x"], c.X["x1"])


@phase("cross0")
def _p(c):
    p_cross(c, 0, c.X["x1"], c.X["x2"])


@phase("ffn0")
def _p(c):
    p_ffn(c, 0, c.X["x2"], c.X["x1"])


@phase("wout1")
def _p(c):
    p_wout(c, 1, c.X["x1"], c.X["x2"])


@phase("cross1")
def _p(c):
    p_cross(c, 1, c.X["x2"], c.X["x1"])


@phase("ffn1")
def _p(c):
    p_ffn(c, 1, c.X["x1"], c.out)


def t5_bucket_np(rel):
    nb = 16
    max_exact = 8
    n = np.abs(rel)
    lr = np.log(np.maximum(n, 1).astype(np.float32) / max_exact) / math.log(128 / max_exact)
    large = np.minimum(max_exact + (lr * (nb - max_exact)).astype(np.int32), nb - 1)
    return np.where(rel > 0, nb, 0) + np.where(n < max_exact, n, large)


def host_consts(S, inputs):
    cst = {}
    cst["c_ident"] = np.eye(128, dtype=np.float32)
    pos = np.arange(S)
    rows, cols = pos // GRID_W, pos % GRID_W
    inv = (10000.0 ** (-np.arange(0, 32, 2, dtype=np.float32) / 32)).astype(np.float32)
    ar_ = rows.astype(np.float32)[:, None] * inv[None, :]
    ac_ = cols.astype(np.float32)[:, None] * inv[None, :]
    cst["c_cos"] = np.concatenate([np.cos(ar_), np.cos(ar_), np.cos(ac_), np.cos(ac_)], axis=1).astype(np.float32)
    cst["c_sin"] = np.concatenate([-np.sin(ar_), np.sin(ar_), -np.sin(ac_), np.sin(ac_)], axis=1).astype(np.float32)
    i = np.arange(128)
    cst["c_trif"] = (i[:, None] <= i[None, :]).astype(np.float32)
    cst["c_trib"] = (i[:, None] >= i[None, :]).astype(np.float32)
    t5 = np.asarray(inputs["t5_table"], np.float32)
    til = np.zeros((4, 6, 128, 512), np.float32)
    for o in range(6):
        k = (o - 1) * 128 + np.arange(128)
        q = np.arange(512)
        idx = t5_bucket_np(k[:, None] - q[None, :])
        for h in range(4):
            til[h, o] = t5[idx, h]
    cst["c_t5"] = til
    far = np.zeros((128, 8), np.float32)
    for h in range(4):
        far[:, h] = t5[15, h]
        far[:, 4 + h] = t5[31, h]
    cst["c_t5far"] = far
    rpb = np.asarray(inputs["od_na_rpb"], np.float32)[0]
    nrows = S // GRID_W
    nab = np.full((8, 3, 8, 128, 512), NEG, np.float32)
    ntile = S // 512
    for var, qt in enumerate([0, min(1, ntile - 1), ntile - 1]):
        for o in range(8):
            kb = 4 * qt - 2 + o
            if kb < 0 or kb >= S // 128:
                continue
            kpos = kb * 128 + np.arange(128)
            qpos = qt * 512 + np.arange(512)
            kr, kc = kpos // GRID_W, kpos % GRID_W
            qr, qc = qpos // GRID_W, qpos % GRID_W
            r0 = np.clip(qr - 4, 0, nrows - 8)
            c0 = np.clip(qc - 8, 0, GRID_W - 16)
            ok = ((kr[:, None] >= r0[None, :]) & (kr[:, None] < r0[None, :] + 8) &
                  (kc[:, None] >= c0[None, :]) & (kc[:, None] < c0[None, :] + 16))
            dr = np.clip(kr[:, None] - qr[None, :] + 7, 0, 14)
            dc = np.clip(kc[:, None] - qc[None, :] + 15, 0, 30)
            for h in range(8):
                nab[h, var, o] = np.where(ok, rpb[h][dr, dc], NEG)
    cst["c_nab"] = nab
    mm = np.zeros((8, 128, 512), np.float32)
    for o in range(4):
        sl = o * 128 + np.arange(128)[:, None]
        tl = np.arange(512)[None, :]
        mm[o] = np.where(sl <= tl, 0.0, NEG)
        mm[4 + o] = np.where(sl >= tl, 0.0, NEG)
    cst["c_mmask"] = mm
    return cst


IN_NAMES = ["norm_mix", "norm_cross", "norm_mem", "norm_ffn", "ev_w_in", "ev_gate_bias", "ev_attn_qk_gain",
            "ev_mlstm_gain", "ev_w_out", "od_w_in", "od_na_qk_gain", "od_diff_qk_gain", "od_diff_lambda",
            "od_diff_gain", "od_w_out", "ca_w_q", "ca_w_kv", "ca_qk_gain", "ca_w_o", "ffn_w_up", "ffn_conv_w",
            "ffn_conv_b", "ffn_w_down"]
SQUEEZE0 = {"ev_w_in", "ev_gate_bias", "ev_attn_qk_gain", "ev_mlstm_gain", "ev_w_out", "od_w_in", "od_na_qk_gain",
            "od_diff_qk_gain", "od_diff_lambda", "od_diff_gain", "od_w_out"}


def make_in_maps(inputs, ncores, S):
    cst = host_consts(S, inputs)
    shared = {}
    for nm in IN_NAMES:
        a = np.ascontiguousarray(np.asarray(inputs[nm], np.float32))
        if nm in SQUEEZE0:
            a = a[0]
        shared[nm] = np.ascontiguousarray(a)
    shared.update(cst)
    maps = []
    for i in range(ncores):
        m = dict(shared)
        m["x"] = np.ascontiguousarray(np.asarray(inputs["x"][i], np.float32))
        m["mem"] = np.ascontiguousarray(np.asarray(inputs["mem"][i], np.float32))
        maps.append(m)
    return maps


def kernel(**inputs):
    x = inputs["x"]
    B, S, _ = x.shape
    nc = build(S)
    maps = make_in_maps(inputs, B, S)
    res = run_bass_kernel_spmd(nc, maps, core_ids=list(range(B)))
    return np.stack([np.asarray(r["out"], np.float32) for r in res.results], axis=0)
```
